# Optimizing a Trainium2 kernel written in Bass

```python
import math
import jax
import jax.numpy as jnp
from jax import lax
import numpy as np


D_MODEL = 2048
BATCH = 4
SEQ = 4096
DEPTH = 1

MEM_LEN = 256
RMS_EPS = 1e-6

A_HEADS = 12
A_HEAD_DIM = 128
A_WIDTH = A_HEADS * A_HEAD_DIM
MOBA_BLOCK = 256
MOBA_TOPK = 3
MOBA_Q_CHUNK = 32

REL_BUCKETS = 32
REL_MAX_DIST = 128

B_HEAD_DIM = 64
B_WIDTH = 1536
B_HEADS = B_WIDTH // B_HEAD_DIM
DECAY_LORA = max(32, int(round(1.8 * math.sqrt(D_MODEL) / 32)) * 32)
AAA_LORA = DECAY_LORA
LNX_EPS = 64e-5

C_HEADS = 4
C_HEAD_DIM = 256
C_WIDTH = C_HEADS * C_HEAD_DIM

N_BRANCHES = 3
IN_SPLITS = (3 * A_WIDTH, A_WIDTH, 3 * B_WIDTH, B_WIDTH, DECAY_LORA, AAA_LORA, C_WIDTH, C_WIDTH, N_BRANCHES * D_MODEL)
IN_WIDTH = sum(IN_SPLITS)

kernel_name = 'hybrid_moba_rwkv7_mem_block'


def rms_norm(x, g):
    xf = x.astype(jnp.float32)
    y = xf * lax.rsqrt(jnp.mean(xf * xf, axis=-1, keepdims=True) + RMS_EPS)
    return (y * g).astype(x.dtype)


def split_heads(t, n_heads):
    b, s, c = t.shape
    return t.reshape(b, s, n_heads, c // n_heads).transpose(0, 2, 1, 3)


def merge_heads(t):
    b, h, s, d = t.shape
    return t.transpose(0, 2, 1, 3).reshape(b, s, h * d)


def t5_bucket(dist):
    n = jnp.maximum(dist, 0)
    max_exact = REL_BUCKETS // 2
    nf = jnp.maximum(n, max_exact).astype(jnp.float32)
    large = max_exact + (jnp.log(nf / max_exact) / math.log(REL_MAX_DIST / max_exact) * (REL_BUCKETS - max_exact)).astype(jnp.int32)
    large = jnp.minimum(large, REL_BUCKETS - 1)
    return jnp.where(n < max_exact, n, large)


def moba_attention(q, k, v, rel_bias):
    bsz, nh, t, dh = q.shape
    f32 = jnp.float32
    nb = -(-t // MOBA_BLOCK)
    tp = nb * MOBA_BLOCK
    pad = ((0, 0), (0, 0), (0, tp - t), (0, 0))
    q = jnp.pad(q, pad)
    k = jnp.pad(k, pad)
    v = jnp.pad(v, pad)
    kb = k.reshape(bsz, nh, nb, MOBA_BLOCK, dh)
    vb = v.reshape(bsz, nh, nb, MOBA_BLOCK, dh)
    k_mean = jnp.mean(kb.astype(f32), axis=3)
    qblk = jnp.arange(tp) // MOBA_BLOCK
    gate = jnp.einsum('bhtd,bhnd->bhtn', q.astype(f32), k_mean)
    fully_past = jnp.arange(nb)[None, :] < qblk[:, None]
    gate = jnp.where(fully_past, gate, -jnp.inf)
    n_sel = min(MOBA_TOPK, nb)
    _, sel = lax.top_k(gate, n_sel)
    sel_ok = sel < qblk[:, None]
    scale = dh ** -0.5
    bias_hb = rel_bias.T.astype(f32)
    head_ix = jnp.arange(nh)[None, :, None, None, None]
    gather_blocks = jax.vmap(jax.vmap(lambda blocks, ix: blocks[ix]))

    def chunk(ci):
        t0 = ci * MOBA_Q_CHUNK
        qc = lax.dynamic_slice_in_dim(q, t0, MOBA_Q_CHUNK, axis=2)
        sel_c = lax.dynamic_slice_in_dim(sel, t0, MOBA_Q_CHUNK, axis=2)
        ok_c = lax.dynamic_slice_in_dim(sel_ok, t0, MOBA_Q_CHUNK, axis=2)
        qpos = t0 + jnp.arange(MOBA_Q_CHUNK)
        k_sel = gather_blocks(kb, sel_c)
        v_sel = gather_blocks(vb, sel_c)
        s_sel = jnp.einsum('bhqd,bhqnpd->bhqnp', qc, k_sel).astype(f32) * scale
        kpos_sel = sel_c[..., None] * MOBA_BLOCK + jnp.arange(MOBA_BLOCK)
        dist_sel = qpos[None, None, :, None, None] - kpos_sel
        s_sel = s_sel + bias_hb[head_ix, t5_bucket(dist_sel)]
        s_sel = jnp.where(ok_c[..., None], s_sel, -jnp.inf)
        own0 = (t0 // MOBA_BLOCK) * MOBA_BLOCK
        k_own = lax.dynamic_slice_in_dim(k, own0, MOBA_BLOCK, axis=2)
        v_own = lax.dynamic_slice_in_dim(v, own0, MOBA_BLOCK, axis=2)
        s_own = jnp.einsum('bhqd,bhpd->bhqp', qc, k_own).astype(f32) * scale
        dist_own = qpos[:, None] - (own0 + jnp.arange(MOBA_BLOCK))[None, :]
        s_own = s_own + bias_hb[:, t5_bucket(dist_own)][None]
        s_own = jnp.where(dist_own >= 0, s_own, -jnp.inf)
        logits = jnp.concatenate([s_sel.reshape(bsz, nh, MOBA_Q_CHUNK, n_sel * MOBA_BLOCK), s_own], axis=-1)
        p = jax.nn.softmax(logits, axis=-1).astype(v.dtype)
        p_sel = p[..., :n_sel * MOBA_BLOCK].reshape(bsz, nh, MOBA_Q_CHUNK, n_sel, MOBA_BLOCK)
        p_own = p[..., n_sel * MOBA_BLOCK:]
        return jnp.einsum('bhqnp,bhqnpd->bhqd', p_sel, v_sel) + jnp.einsum('bhqp,bhpd->bhqd', p_own, v_own)

    out = lax.map(chunk, jnp.arange(tp // MOBA_Q_CHUNK))
    out = out.transpose(1, 2, 0, 3, 4).reshape(bsz, nh, tp, dh)
    return out[:, :, :t]


def token_shift_lerp(p, mu):
    prev = jnp.pad(p, ((0, 0), (1, 0), (0, 0)))[:, :-1]
    return p + (prev - p) * mu


def wkv7_scan(r, decay, k, v, a_vec, b_vec):
    bsz, t, nh, n = r.shape

    def step(state, inp):
        r_t, w_t, k_t, v_t, a_t, b_t = inp
        sa = jnp.einsum('bhvk,bhk->bhv', state, a_t)
        state = state * w_t[:, :, None, :] + sa[..., None] * b_t[:, :, None, :] + v_t[..., None] * k_t[:, :, None, :]
        return state, jnp.einsum('bhvk,bhk->bhv', state, r_t)

    xs = (jnp.moveaxis(r, 1, 0), jnp.moveaxis(decay, 1, 0), jnp.moveaxis(k, 1, 0), jnp.moveaxis(v, 1, 0), jnp.moveaxis(a_vec, 1, 0), jnp.moveaxis(b_vec, 1, 0))
    s0 = jnp.zeros((bsz, nh, n, n), jnp.float32)
    _, y = lax.scan(step, s0, xs)
    return jnp.moveaxis(y, 0, 1)


def rwkv7_time_mix(r, k, v, lw, la, mu_r, mu_k, mu_v, mu_w, mu_a, w0, w_decay2, a0, w_aaa2, k_k, k_a, r_k, lnx_w, lnx_b):
    bsz, t, _ = r.shape
    f32 = jnp.float32
    r = token_shift_lerp(r, mu_r)
    k = token_shift_lerp(k, mu_k)
    v = token_shift_lerp(v, mu_v)
    lw = token_shift_lerp(lw, mu_w)
    la = token_shift_lerp(la, mu_a)
    w_log = -jax.nn.softplus(-(w0 + jnp.tanh(lw) @ w_decay2).astype(f32)) - 0.5
    decay = jnp.exp(-jnp.exp(w_log))
    a = jax.nn.sigmoid((a0 + la @ w_aaa2).astype(f32))

    def heads(z):
        return z.astype(f32).reshape(bsz, t, B_HEADS, B_HEAD_DIM)

    kk = heads(k * k_k)
    kk = kk / jnp.maximum(jnp.sqrt(jnp.sum(kk * kk, axis=-1, keepdims=True)), 1e-12)
    k_mod = k.astype(f32) * (1.0 + (a - 1.0) * k_a)
    rh = heads(r)
    kh = heads(k_mod)
    vh = heads(v)
    ah = heads(a)
    y = wkv7_scan(rh, heads(decay), kh, vh, -kk, kk * ah)
    mean = jnp.mean(y, axis=-1, keepdims=True)
    var = jnp.mean(jnp.square(y - mean), axis=-1, keepdims=True)
    y = ((y - mean) * lax.rsqrt(var + LNX_EPS)).reshape(bsz, t, B_WIDTH) * lnx_w + lnx_b
    bonus = jnp.sum(rh * kh * r_k, axis=-1, keepdims=True) * vh
    return (y + bonus.reshape(bsz, t, B_WIDTH)).astype(r.dtype)


def memory_cross_attention(q, mem_n, w_mem_kv):
    k_m, v_m = jnp.split(mem_n @ w_mem_kv, 2, axis=-1)
    qh = split_heads(q, C_HEADS)
    kh = split_heads(k_m, C_HEADS)
    vh = split_heads(v_m, C_HEADS)
    s = jnp.einsum('bhtd,bhmd->bhtm', qh, kh).astype(jnp.float32) * (C_HEAD_DIM ** -0.5)
    p = jax.nn.softmax(s, axis=-1).astype(vh.dtype)
    return merge_heads(jnp.einsum('bhtm,bhmd->bhtd', p, vh))


def hybrid_layer(x, mem, rel_bias, norm_g, mem_norm_g, w_in, rw_mu_r, rw_mu_k, rw_mu_v, rw_mu_w, rw_mu_a, rw_w0, rw_w_decay2, rw_a0, rw_w_aaa2, rw_k_k, rw_k_a, rw_r_k, rw_lnx_w, rw_lnx_b, w_mem_kv, w_proj_a, w_proj_b, w_proj_c, w_out):
    h = rms_norm(x, norm_g)
    mem_n = rms_norm(mem, mem_norm_g)
    proj = h @ w_in
    cuts = [int(c) for c in np.cumsum(IN_SPLITS)[:-1]]
    qkv_a, z_a, rkv_b, z_b, lw, la, q_c, z_c, gates = jnp.split(proj, cuts, axis=-1)
    q_a, k_a, v_a = jnp.split(qkv_a, 3, axis=-1)
    y_a = merge_heads(moba_attention(split_heads(q_a, A_HEADS), split_heads(k_a, A_HEADS), split_heads(v_a, A_HEADS), rel_bias))
    y_a = y_a * jax.nn.silu(z_a)
    r_b, k_b, v_b = jnp.split(rkv_b, 3, axis=-1)
    y_b = rwkv7_time_mix(r_b, k_b, v_b, lw, la, rw_mu_r, rw_mu_k, rw_mu_v, rw_mu_w, rw_mu_a, rw_w0, rw_w_decay2, rw_a0, rw_w_aaa2, rw_k_k, rw_k_a, rw_r_k, rw_lnx_w, rw_lnx_b)
    y_b = y_b * jax.nn.silu(z_b)
    y_c = memory_cross_attention(q_c, mem_n, w_mem_kv) * jax.nn.silu(z_c)
    g_a, g_b, g_c = jnp.split(jax.nn.sigmoid(gates), N_BRANCHES, axis=-1)
    merged = g_a * (y_a @ w_proj_a) + g_b * (y_b @ w_proj_b) + g_c * (y_c @ w_proj_c)
    return x + merged @ w_out


def setup_inputs(seed: int = 0) -> dict:
    key = jax.random.key(seed)
    ks = jax.random.split(key, 32)
    L = DEPTH
    nrm = jax.random.normal
    f32 = jnp.float32
    return {
        'x': nrm(ks[0], (BATCH, SEQ, D_MODEL), f32),
        'mem': nrm(ks[1], (BATCH, MEM_LEN, D_MODEL), f32),
        'rel_bias': 0.2 * nrm(ks[2], (REL_BUCKETS, A_HEADS), f32),
        'norm_g': 1.0 + 0.02 * nrm(ks[3], (L, D_MODEL), f32),
        'mem_norm_g': 1.0 + 0.02 * nrm(ks[4], (L, D_MODEL), f32),
        'w_in': nrm(ks[5], (L, D_MODEL, IN_WIDTH), f32) * D_MODEL ** -0.5,
        'rw_mu_r': jax.random.uniform(ks[6], (L, B_WIDTH), f32),
        'rw_mu_k': jax.random.uniform(ks[7], (L, B_WIDTH), f32),
        'rw_mu_v': jax.random.uniform(ks[8], (L, B_WIDTH), f32),
        'rw_mu_w': jax.random.uniform(ks[9], (L, DECAY_LORA), f32),
        'rw_mu_a': jax.random.uniform(ks[10], (L, AAA_LORA), f32),
        'rw_w0': jax.random.uniform(ks[11], (L, B_WIDTH), f32, -5.0, 0.5),
        'rw_w_decay2': 0.1 * nrm(ks[12], (L, DECAY_LORA, B_WIDTH), f32) * DECAY_LORA ** -0.5,
        'rw_a0': 0.1 * nrm(ks[13], (L, B_WIDTH), f32),
        'rw_w_aaa2': 0.5 * nrm(ks[14], (L, AAA_LORA, B_WIDTH), f32) * AAA_LORA ** -0.5,
        'rw_k_k': 0.85 + 0.05 * nrm(ks[15], (L, B_WIDTH), f32),
        'rw_k_a': 1.0 + 0.05 * nrm(ks[16], (L, B_WIDTH), f32),
        'rw_r_k': 0.1 * nrm(ks[17], (L, B_HEADS, B_HEAD_DIM), f32),
        'rw_lnx_w': 1.0 + 0.02 * nrm(ks[18], (L, B_WIDTH), f32),
        'rw_lnx_b': 0.02 * nrm(ks[19], (L, B_WIDTH), f32),
        'w_mem_kv': nrm(ks[20], (L, D_MODEL, 2 * C_WIDTH), f32) * D_MODEL ** -0.5,
        'w_proj_a': nrm(ks[21], (L, A_WIDTH, D_MODEL), f32) * A_WIDTH ** -0.5,
        'w_proj_b': nrm(ks[22], (L, B_WIDTH, D_MODEL), f32) * B_WIDTH ** -0.5,
        'w_proj_c': nrm(ks[23], (L, C_WIDTH, D_MODEL), f32) * C_WIDTH ** -0.5,
        'w_out': nrm(ks[24], (L, D_MODEL, D_MODEL), f32) * D_MODEL ** -0.5,
        'final_norm_g': 1.0 + 0.02 * nrm(ks[25], (D_MODEL,), f32),
    }


def reference(x, mem, rel_bias, norm_g, mem_norm_g, w_in, rw_mu_r, rw_mu_k, rw_mu_v, rw_mu_w, rw_mu_a, rw_w0, rw_w_decay2, rw_a0, rw_w_aaa2, rw_k_k, rw_k_a, rw_r_k, rw_lnx_w, rw_lnx_b, w_mem_kv, w_proj_a, w_proj_b, w_proj_c, w_out, final_norm_g):
    for layer in range(DEPTH):
        x = hybrid_layer(x, mem, rel_bias, norm_g[layer], mem_norm_g[layer], w_in[layer], rw_mu_r[layer], rw_mu_k[layer], rw_mu_v[layer], rw_mu_w[layer], rw_mu_a[layer], rw_w0[layer], rw_w_decay2[layer], rw_a0[layer], rw_w_aaa2[layer], rw_k_k[layer], rw_k_a[layer], rw_r_k[layer], rw_lnx_w[layer], rw_lnx_b[layer], w_mem_kv[layer], w_proj_a[layer], w_proj_b[layer], w_proj_c[layer], w_out[layer])
    return rms_norm(x, final_norm_g)
```

```python
import numpy as np
import ml_dtypes
import concourse.bass as bass
import concourse.mybir as mybir
from concourse.bass_utils import run_bass_kernel_spmd
from contextlib import ExitStack

F32 = mybir.dt.float32
BF16 = mybir.dt.bfloat16
AF = mybir.ActivationFunctionType
ALU = mybir.AluOpType
AX = mybir.AxisListType

ENG = ('pe', 'act', 'dve', 'pool', 'sp')
BLOCKNAME = {'pe': 'tensor', 'act': 'scalar', 'dve': 'vector', 'pool': 'gpsimd', 'sp': 'sync'}
EPOCH = 30000


class Sched:
    def __init__(self, nc):
        self.nc = nc
        self.ops = {e: [] for e in ENG}
        self.nsig = {e: 0 for e in ENG}
        self.res = {}
        self.waited = {e: {} for e in ENG}
        self.dma_cnt = {}
        self.semkeys = {}

    def _ev_next(self, eng):
        n = self.nsig[eng] + 1
        ep = (n - 1) // EPOCH
        return ((eng, ep), n - ep * EPOCH)

    def _deps(self, eng, reads, writes):
        waits = {}

        def need(k, v):
            if waits.get(k, 0) < v:
                waits[k] = v
        for key in reads:
            r = self.res.get(key)
            if r and r[0] is not None:
                need(*r[0])
        for key in writes:
            r = self.res.get(key)
            if r:
                if r[0] is not None:
                    need(*r[0])
                for k, v in r[1].items():
                    need(k, v)
        out = []
        nk, nv = self._ev_next(eng)
        for k, v in waits.items():
            if k == nk and v >= nv:
                continue
            if self.waited[eng].get(k, 0) < v:
                self.waited[eng][k] = v
                out.append((k, v))
        return out

    def _record(self, ev, reads, writes):
        for key in reads:
            r = self.res.setdefault(key, [None, {}])
            if r[1].get(ev[0], 0) < ev[1]:
                r[1][ev[0]] = ev[1]
        for key in writes:
            self.res[key] = [ev, {}]

    def op(self, eng, fn, reads=(), writes=(), sig=True):
        waits = self._deps(eng, reads, writes)
        ev = self._ev_next(eng)
        if sig:
            self.nsig[eng] += 1
            self.semkeys[ev[0]] = 1
            self.ops[eng].append((fn, waits, ev[0], 1))
        else:
            self.ops[eng].append((fn, waits, None, 0))
        self._record(ev, reads, writes)

    def dma(self, q, fn, reads, writes, stream, inc=16):
        waits = self._deps(q, reads, writes)
        semkey = ('dma', stream)
        c = self.dma_cnt.get(semkey, 0) + inc
        self.dma_cnt[semkey] = c
        self.semkeys[semkey] = 1
        self.ops[q].append((fn, waits, semkey, inc))
        self._record((semkey, c), reads, writes)

    def barrier(self):
        evs = []
        for e in ENG:
            n = self.nsig[e]
            if n == 0:
                continue
            ep = (n - 1) // EPOCH
            evs.append(((e, ep), n - ep * EPOCH))
        for k, c in self.dma_cnt.items():
            evs.append((k, c))
        for e in ENG:
            waits = []
            for k, v in evs:
                if k[0] == e:
                    continue
                if self.waited[e].get(k, 0) < v:
                    self.waited[e][k] = v
                    waits.append((k, v))
            self.ops[e].append((None, waits, None, 0))
        self.res = {}

    def finish(self):
        waits = []
        for k, c in self.dma_cnt.items():
            if self.waited['sp'].get(k, 0) < c:
                waits.append((k, c))
        self.ops['sp'].append((None, waits, None, 0))

    def emit(self):
        nc = self.nc
        with ExitStack() as st:
            sems = {}
            for i, k in enumerate(self.semkeys):
                sems[k] = st.enter_context(nc.semaphore("s%d" % i))
            block = st.enter_context(nc.Block())
            for eng in ENG:
                def body(e, eng=eng):
                    for fn, waits, semkey, inc in self.ops[eng]:
                        for (k, v) in waits:
                            e.wait_ge(sems[k], v)
                        if fn is None:
                            continue
                        ins = fn(e)
                        if semkey is not None:
                            ins.then_inc(sems[semkey], inc)
                getattr(block, BLOCKNAME[eng])(body)


D = 2048
T = 4096
NH_A = 6
NH_B = 12
NH_C = 2
CW = 768
CC = 512
MEM = 256
NC1 = 8 * CW + 192 + 2 * CC
TB = 512
NTB = T // TB
RMS_EPS = 1e-6
LNX_EPS = 64e-5

SEGS = []
_o = 0
for _n, _w, _k in (('qa', CW, 'q'), ('ka', CW, 'k'), ('va', CW, 'v'), ('za', CW, 'silu'),
                   ('rb', CW, 'shift'), ('kb', CW, 'shift'), ('vb', CW, 'shift'), ('zb', CW, 'silu'),
                   ('lw', 96, 'shift'), ('la', 96, 'shift'), ('qc', CC, 'q'), ('zc', CC, 'silu')):
    SEGS.append((_n, _o, _w, _k))
    _o += _w
assert _o == NC1


def build(stage=99, dbg=False):
    MOBA = True
    RWKV = True
    import os
    BIS = int(os.environ.get('BIS', '9'))
    nc = bass.Bass("TRN2", target_bir_lowering=False)
    S = Sched(nc)
    P = 128

    def din(name, shape, dt=F32):
        return nc.dram_tensor(name, list(shape), dt, kind="ExternalInput")

    def dscr(name, shape, dt):
        if dbg:
            return nc.dram_tensor(name, list(shape), dt, kind="ExternalOutput")
        return nc.dram_tensor(name, list(shape), dt)

    x_d = din("x", [T, D])
    mem_d = din("mem", [MEM, D])
    w1_d = din("w1", [D, NC1])
    wg_d = din("wg", [D, 3 * D])
    ng_d = din("norm_g", [1, D])
    mng_d = din("mem_norm_g", [1, D])
    fng_d = din("final_norm_g", [1, D])
    ident_d = din("ident_in", [P, P], F32)
    out_d = nc.dram_tensor("out", [T // 2, D], F32, kind="ExternalOutput")

    scr = {}
    scr['qa'] = dscr("s_qa", [CW, T], BF16)
    scr['ka'] = dscr("s_ka", [CW, T], BF16)
    scr['va'] = dscr("s_va", [T, CW], BF16)
    scr['za'] = dscr("s_za", [CW, T], BF16)
    scr['rb'] = dscr("s_rb", [CW, T], F32)
    scr['kb'] = dscr("s_kb", [CW, T], F32)
    scr['vb'] = dscr("s_vb", [CW, T], F32)
    scr['zb'] = dscr("s_zb", [CW, T], BF16)
    scr['lw'] = dscr("s_lw", [96, T], F32)
    scr['la'] = dscr("s_la", [96, T], F32)
    scr['qc'] = dscr("s_qc", [CC, T], BF16)
    scr['zc'] = dscr("s_zc", [CC, T], BF16)
    scr['g'] = dscr("s_g", [3 * D, T], BF16)
    hhv_d = din("hhv", [128, 2], F32)

    wmkv_d = din("wmkv", [D, 1024])
    kmT_d = dscr("s_kmT", [CC, MEM], BF16)
    vm_d = dscr("s_vm", [MEM, CC], BF16)
    ones_d = din("ones_in", [P, P], F32)
    dbg_out = {}

    PS = [nc.alloc_psum_tensor("ps%d" % i, [P, 512], F32) for i in range(8)]

    ident = nc.alloc_sbuf_tensor("ident", [P, P], BF16)
    identf = nc.alloc_sbuf_tensor("identf", [P, P], F32)
    S.dma('sp', lambda e: e.dma_start(out=identf[:], in_=ident_d.ap()), [], ['identf'], 'c0')
    S.op('dve', lambda e: e.tensor_copy(out=ident[:], in_=identf[:]), ['identf'], ['ident'])

    with ExitStack() as ph:
        hT = ph.enter_context(nc.sbuf_tensor("hT", [P, 16, T], BF16))
        gbc = ph.enter_context(nc.sbuf_tensor("gbc", [P, D], F32))
        xt = [ph.enter_context(nc.sbuf_tensor("xt%d" % i, [P, D], F32)) for i in range(2)]
        xn = [ph.enter_context(nc.sbuf_tensor("xn%d" % i, [P, D], BF16)) for i in range(2)]
        junk = ph.enter_context(nc.sbuf_tensor("junk", [P, D], BF16))
        ss = [ph.enter_context(nc.sbuf_tensor("ss%d" % i, [P, 2], F32)) for i in range(2)]

        S.dma('sp', lambda e: e.dma_start(out=gbc[:], in_=ng_d.ap().partition_broadcast(P)), [], ['gbc'], 'c1')

        def norm_T(src_ap, ntile, dst, gkey, tag):
            for tt in range(ntile):
                b = tt % 2
                S.dma('sp', lambda e, tt=tt, b=b: e.dma_start(out=xt[b][:], in_=src_ap[tt * P:(tt + 1) * P, :]),
                      [], [('xt', b)], 'xt%d' % b)
                S.op('act', lambda e, b=b: e.activation(out=junk[:], in_=xt[b][:], func=AF.Square,
                                                        accum_out=ss[b][:, 0:1]),
                     [('xt', b)], [('ss', b), 'junk'])
                if BIS < 2:
                    continue
                S.op('dve', lambda e, b=b: e.tensor_scalar(out=ss[b][:, 1:2], in0=ss[b][:, 0:1], scalar1=1.0 / D,
                                                           scalar2=RMS_EPS, op0=ALU.mult, op1=ALU.add),
                     [('ss', b)], [('ss', b)])
                S.op('act', lambda e, b=b: e.activation(out=ss[b][:, 1:2], in_=ss[b][:, 1:2], func=AF.Sqrt),
                     [('ss', b)], [('ss', b)])
                S.op('dve', lambda e, b=b: e.reciprocal(out=ss[b][:, 1:2], in_=ss[b][:, 1:2]),
                     [('ss', b)], [('ss', b)])
                S.op('dve', lambda e, b=b: e.scalar_tensor_tensor(out=xn[b][:], in0=xt[b][:], scalar=ss[b][:, 1:2],
                                                                  in1=gbc[:], op0=ALU.mult, op1=ALU.mult),
                     [('xt', b), ('ss', b), gkey], [('xn', b)])
                for half in range(2 if BIS >= 3 else 0):
                    pb = PS[(tt * 2 + half) % 4]
                    pbv = pb.bitcast(BF16)
                    for k8 in range(8):
                        k = half * 8 + k8
                        S.op('pe', lambda e, b=b, k=k, k8=k8, pbv=pbv: e.transpose(
                            out=pbv[:, k8 * P:(k8 + 1) * P], in_=xn[b][:, k * P:(k + 1) * P], identity=ident[:]),
                            [('xn', b), 'ident'], [('ps', (tt * 2 + half) % 4)], sig=(k8 == 7))
                    eng = 'act' if half == 0 else 'pool_no'
                    eng = 'act' if half == 0 else 'dve'
                    src = pbv[:, 0:1024].rearrange("p (k t) -> p k t", k=8)
                    dstv = dst[:, half * 8:(half + 1) * 8, tt * P:(tt + 1) * P]
                    if eng == 'act':
                        S.op('act', lambda e, src=src, dstv=dstv: e.activation(out=dstv, in_=src, func=AF.Copy),
                             [('ps', (tt * 2 + half) % 4)], [(tag, tt)])
                    else:
                        S.op('dve', lambda e, src=src, dstv=dstv: e.tensor_copy(out=dstv, in_=src),
                             [('ps', (tt * 2 + half) % 4)], [(tag, tt)])

        norm_T(x_d.ap(), int(os.environ.get('NT', T // P)), hT, 'gbc', 'hT')

        if dbg:
            dbg_out['hT'] = nc.dram_tensor("dbg_hT", [P, 16, T], BF16, kind="ExternalOutput")
            S.dma('sp', lambda e: e.dma_start(out=dbg_out['hT'].ap(), in_=hT[:]),
                  [('hT', tt) for tt in range(T // P)], ['dbg_hT'], 'dbg')

        WG = 256
        NWB = 2
        wbuf = [ph.enter_context(nc.sbuf_tensor("wb%d" % i, [P, 16, WG], BF16)) for i in range(NWB)]
        NST = 4
        stg32 = [ph.enter_context(nc.sbuf_tensor("st32_%d" % i, [P, TB + 1], F32)) for i in range(NST)]
        stg16 = [ph.enter_context(nc.sbuf_tensor("st16_%d" % i, [P, TB], BF16)) for i in range(NST)]
        tmp32 = [ph.enter_context(nc.sbuf_tensor("tm32_%d" % i, [P, TB], F32)) for i in range(2)]
        mu_sb = ph.enter_context(nc.sbuf_tensor("mu_sb", [P, 20], F32))
        mu_d = din("mu1", [P, 20])
        S.dma('sp', lambda e: e.dma_start(out=mu_sb[:], in_=mu_d.ap()), [], ['mu'], 'c2')

        wcnt = [0]
        scnt = [0]
        pcnt = [0]

        def load_w(src_d, c0, cw):
            i = wcnt[0] % NWB
            wcnt[0] += 1
            srcv = src_d.ap()[:, c0:c0 + cw].rearrange("(k p) c -> p k c", p=P)
            S.dma('pool', lambda e: e.dma_start(out=wbuf[i][:, :, 0:cw], in_=srcv), [], [('wb', i)], 'wb%d' % i)
            return i

        def proj_group(src_d, c0, cw, kind, name, seg_off, tbs, mucol0):
            wi = load_w(src_d, c0, cw)
            if kind == 'v':
                for tb in tbs:
                    for t4 in range(4):
                        tt = tb * 4 + t4
                        pi = 4 + pcnt[0] % 4
                        pcnt[0] += 1
                        for k in range(16):
                            S.op('pe', lambda e, k=k, tt=tt, pi=pi: e.matmul(
                                PS[pi][:, 0:cw], lhsT=hT[:, k, tt * P:(tt + 1) * P], rhs=wbuf[wi][:, k, 0:cw],
                                start=(k == 0), stop=(k == 15)),
                                [('wb', wi)], [('ps', pi)], sig=(k == 15))
                        si = scnt[0] % NST
                        scnt[0] += 1
                        S.op('act', lambda e, pi=pi, si=si: e.activation(out=stg16[si][:, 0:cw], in_=PS[pi][:, 0:cw],
                                                                         func=AF.Copy),
                             [('ps', pi)], [('s16', si)])
                        S.dma('sp', lambda e, si=si, tt=tt: e.dma_start(
                            out=scr[name].ap()[tt * P:(tt + 1) * P, seg_off:seg_off + cw], in_=stg16[si][:, 0:cw]),
                            [('s16', si)], [(name, 'v', tt, seg_off)], 'st%d' % si)
                return
            nsub = (cw + P - 1) // P
            for sub in range(nsub):
                m = min(P, cw - sub * P)
                row0 = seg_off + sub * P
                for tb in tbs:
                    pi = 4 + pcnt[0] % 4
                    pcnt[0] += 1
                    for k in range(16):
                        S.op('pe', lambda e, k=k, tb=tb, pi=pi, sub=sub, m=m: e.matmul(
                            PS[pi][0:m, :], lhsT=wbuf[wi][:, k, sub * P:sub * P + m], rhs=hT[:, k, tb * TB:(tb + 1) * TB],
                            start=(k == 0), stop=(k == 15)),
                            [('wb', wi)], [('ps', pi)], sig=(k == 15))
                    si = scnt[0] % NST
                    scnt[0] += 1
                    if kind in ('q', 'k', 'silu', 'sig'):
                        if kind == 'q':
                            sc = 128 ** -0.5 if name == 'qa' else 256 ** -0.5
                            f = lambda e, pi=pi, si=si, m=m, sc=sc: e.activation(
                                out=stg16[si][0:m, :], in_=PS[pi][0:m, :], func=AF.Copy, scale=sc)
                        elif kind == 'k':
                            f = lambda e, pi=pi, si=si, m=m: e.activation(
                                out=stg16[si][0:m, :], in_=PS[pi][0:m, :], func=AF.Copy)
                        elif kind == 'silu':
                            f = lambda e, pi=pi, si=si, m=m: e.activation(
                                out=stg16[si][0:m, :], in_=PS[pi][0:m, :], func=AF.Silu)
                        else:
                            f = lambda e, pi=pi, si=si, m=m: e.activation(
                                out=stg16[si][0:m, :], in_=PS[pi][0:m, :], func=AF.Sigmoid)
                        S.op('act', f, [('ps', pi)], [('s16', si)])
                        tcol = tb * TB
                        S.dma('sp', lambda e, si=si, m=m, row0=row0, tcol=tcol: e.dma_start(
                            out=scr[name].ap()[row0:row0 + m, tcol:tcol + TB], in_=stg16[si][0:m, :]),
                            [('s16', si)], [(name, row0, tcol)], 'st%d' % si)
                    else:
                        mucol = mucol0 + sub
                        first = (tb == tbs[0])
                        prev_si = (scnt[0] - 2) % NST
                        if first:
                            S.op('dve', lambda e, si=si, m=m: e.memset(stg32[si][0:m, 0:1], 0.0), [], [('s32', si)])
                        else:
                            S.op('dve', lambda e, si=si, m=m, ps_=prev_si: e.tensor_copy(
                                out=stg32[si][0:m, 0:1], in_=stg32[ps_][0:m, TB:TB + 1]),
                                [('s32c', prev_si)], [('s32', si)])
                        S.op('act', lambda e, pi=pi, si=si, m=m: e.activation(
                            out=stg32[si][0:m, 1:TB + 1], in_=PS[pi][0:m, :], func=AF.Copy),
                            [('ps', pi)], [('s32', si), ('s32c', si)])
                        tb2 = si % 2
                        S.op('dve', lambda e, si=si, m=m, tb2=tb2: e.tensor_sub(
                            out=tmp32[tb2][0:m, :], in0=stg32[si][0:m, 0:TB], in1=stg32[si][0:m, 1:TB + 1]),
                            [('s32', si), ('s32c', si)], [('t32', tb2)])
                        S.op('dve', lambda e, si=si, m=m, tb2=tb2, mucol=mucol: e.scalar_tensor_tensor(
                            out=tmp32[tb2][0:m, :], in0=tmp32[tb2][0:m, :], scalar=mu_sb[0:m, mucol:mucol + 1],
                            in1=stg32[si][0:m, 1:TB + 1], op0=ALU.mult, op1=ALU.add),
                            [('t32', tb2), ('s32', si), ('s32c', si), 'mu'], [('t32', tb2)])
                        S.dma('sp', lambda e, m=m, row0=row0, tb=tb, tb2=tb2: e.dma_start(
                            out=scr[name].ap()[row0:row0 + m, tb * TB:(tb + 1) * TB], in_=tmp32[tb2][0:m, :]),
                            [('t32', tb2)], [(name, row0, tb * TB)], 'tm%d' % tb2)

        if stage >= 1:
            memT = ph.enter_context(nc.sbuf_tensor("memT", [P, 16, MEM], BF16))
            S.dma('sp', lambda e: e.dma_start(out=gbc[:], in_=mng_d.ap().partition_broadcast(P)), [], ['gbc'], 'c1')
            norm_T(mem_d.ap(), MEM // P, memT, 'gbc', 'memT')
            memkeys = [('memT', tt) for tt in range(MEM // P)]
            for c0 in (0, 256):
                wi = load_w(wmkv_d, c0, 256)
                for sub in range(2):
                    pi = 4 + pcnt[0] % 4
                    pcnt[0] += 1
                    for k in range(16):
                        S.op('pe', lambda e, k=k, pi=pi, sub=sub, wi=wi: e.matmul(
                            PS[pi][:, 0:MEM], lhsT=wbuf[wi][:, k, sub * P:(sub + 1) * P], rhs=memT[:, k, :],
                            start=(k == 0), stop=(k == 15)), [('wb', wi)] + memkeys, [('ps', pi)], sig=(k == 15))
                    si = scnt[0] % NST
                    scnt[0] += 1
                    S.op('act', lambda e, pi=pi, si=si: e.activation(out=stg16[si][:, 0:MEM], in_=PS[pi][:, 0:MEM],
                                                                     func=AF.Copy), [('ps', pi)], [('s16', si)])
                    S.dma('sp', lambda e, si=si, r0=c0 + sub * P: e.dma_start(
                        out=kmT_d.ap()[r0:r0 + P, :], in_=stg16[si][:, 0:MEM]), [('s16', si)], [('kmT', c0, sub)],
                        'st%d' % si)
            for c0 in (512, 768):
                wi = load_w(wmkv_d, c0, 256)
                for mt in range(2):
                    pi = 4 + pcnt[0] % 4
                    pcnt[0] += 1
                    for k in range(16):
                        S.op('pe', lambda e, k=k, pi=pi, mt=mt, wi=wi: e.matmul(
                            PS[pi][:, 0:256], lhsT=memT[:, k, mt * P:(mt + 1) * P], rhs=wbuf[wi][:, k, 0:256],
                            start=(k == 0), stop=(k == 15)), [('wb', wi)] + memkeys, [('ps', pi)], sig=(k == 15))
                    si = scnt[0] % NST
                    scnt[0] += 1
                    S.op('act', lambda e, pi=pi, si=si: e.activation(out=stg16[si][:, 0:256], in_=PS[pi][:, 0:256],
                                                                     func=AF.Copy), [('ps', pi)], [('s16', si)])
                    S.dma('sp', lambda e, si=si, mt=mt, c0=c0: e.dma_start(
                        out=vm_d.ap()[mt * P:(mt + 1) * P, c0 - 512:c0 - 256], in_=stg16[si][:, 0:256]),
                        [('s16', si)], [('vm', c0, mt)], 'st%d' % si)

        mucols = {'rb': 0, 'kb': 6, 'vb': 12, 'lw': 18, 'la': 19}
        for (name, off, width, kind) in SEGS:
            if stage < 1:
                break
            c = 0
            while c < width:
                cw = min(WG, width - c)
                proj_group(w1_d, off + c, cw, kind, name, c, list(range(NTB)), mucols.get(name, 0) + c // P)
                c += cw
        if stage >= 2:
            c = 0
            while c < 3 * D:
                proj_group(wg_d, c, WG, 'k' if False else 'sig', 'g', c, list(range(NTB)), 0)
                c += WG

    S.barrier()

    yown = dscr("s_yown", [D, T], BF16)
    ygs = [nc.dram_tensor("s_yg%d" % j, [2 * P, T], BF16) for j in range(D // P)]
    ones_bf = nc.alloc_sbuf_tensor("ones_bf", [P, P], BF16)
    onesf = nc.alloc_sbuf_tensor("onesf", [P, P], F32)
    S.dma('sp', lambda e: e.dma_start(out=onesf[:], in_=ones_d.ap()), [], ['onesf'], 'c0')
    S.op('dve', lambda e: e.tensor_copy(out=ones_bf[:], in_=onesf[:]), ['onesf'], ['ones'])
    NYZ = 12

    if stage >= 3:
        with ExitStack() as ph:
            kmT = ph.enter_context(nc.sbuf_tensor("kmT_sb", [P, 4, MEM], BF16))
            vm = ph.enter_context(nc.sbuf_tensor("vm_sb", [P, 2, CC], BF16))
            qcs = [ph.enter_context(nc.sbuf_tensor("qcs%d" % i, [P, 4, TB], BF16)) for i in range(2)]
            zcs = [ph.enter_context(nc.sbuf_tensor("zcs%d" % i, [P, 4, TB], BF16)) for i in range(2)]
            pT = [ph.enter_context(nc.sbuf_tensor("pTc%d" % i, [P, 2, TB], BF16)) for i in range(2)]
            rec = [ph.enter_context(nc.sbuf_tensor("recc%d" % i, [P, TB], F32)) for i in range(2)]
            tq = [ph.enter_context(nc.sbuf_tensor("tqc%d" % i, [P, TB], F32)) for i in range(2)]
            yo = [ph.enter_context(nc.sbuf_tensor("yoc%d" % i, [P, TB], BF16)) for i in range(2)]
            S.dma('sp', lambda e: e.dma_start(out=kmT[:], in_=kmT_d.ap().rearrange("(c p) m -> p c m", p=P)),
                  [], ['kmTs'], 'c5')
            S.dma('sp', lambda e: e.dma_start(out=vm[:], in_=vm_d.ap().rearrange("(t p) c -> p t c", p=P)),
                  [], ['vms'], 'c6')
            itc = 0
            for tb in range(NTB):
                b = tb % 2
                S.dma('sp', lambda e, b=b, tb=tb: e.dma_start(
                    out=qcs[b][:], in_=scr['qc'].ap()[:, tb * TB:(tb + 1) * TB].rearrange("(c p) t -> p c t", p=P)),
                    [], [('qcs', b)], 'qcs%d' % b)
                S.dma('sp', lambda e, b=b, tb=tb: e.dma_start(
                    out=zcs[b][:], in_=scr['zc'].ap()[:, tb * TB:(tb + 1) * TB].rearrange("(c p) t -> p c t", p=P)),
                    [], [('zcs', b)], 'zcs%d' % b)
                for h in range(NH_C):
                    pb = itc % 2
                    itc += 1
                    for mt in range(2):
                        for dc in range(2):
                            S.op('pe', lambda e, mt=mt, dc=dc, h=h, b=b: e.matmul(
                                PS[mt][:, :], lhsT=kmT[:, h * 2 + dc, mt * P:(mt + 1) * P], rhs=qcs[b][:, h * 2 + dc, :],
                                start=(dc == 0), stop=(dc == 1)), ['kmTs', ('qcs', b)], [('ps', mt)], sig=(dc == 1))
                        S.op('act', lambda e, mt=mt, pb=pb: e.activation(out=pT[pb][:, mt, :], in_=PS[mt][:, :],
                                                                         func=AF.Exp), [('ps', mt)], [('pTc', pb, mt)])
                    for mt in range(2):
                        S.op('pe', lambda e, mt=mt, pb=pb: e.matmul(
                            PS[4][:, :], lhsT=ones_bf[:], rhs=pT[pb][:, mt, :], start=(mt == 0), stop=(mt == 1)),
                            ['ones', ('pTc', pb, 0), ('pTc', pb, 1)], [('ps', 4)], sig=(mt == 1))
                    S.op('dve', lambda e, pb=pb: e.reciprocal(out=rec[pb][:], in_=PS[4][:, :]), [('ps', 4)], [('recc', pb)])
                    for dc in range(2):
                        for mt in range(2):
                            S.op('pe', lambda e, mt=mt, dc=dc, h=h, pb=pb: e.matmul(
                                PS[2 + dc][:, :], lhsT=vm[:, mt, h * 256 + dc * P:h * 256 + (dc + 1) * P],
                                rhs=pT[pb][:, mt, :], start=(mt == 0), stop=(mt == 1)),
                                ['vms', ('pTc', pb, 0), ('pTc', pb, 1)], [('ps', 2 + dc)], sig=(mt == 1))
                        yb = (itc * 2 + dc) % 2
                        S.op('dve', lambda e, dc=dc, pb=pb, yb=yb: e.tensor_mul(
                            out=tq[yb][:], in0=PS[2 + dc][:, :], in1=rec[pb][:]), [('ps', 2 + dc), ('recc', pb)], [('tqc', yb)])
                        S.op('pool', lambda e, dc=dc, h=h, b=b, yb=yb: e.tensor_mul(
                            out=yo[yb][:], in0=tq[yb][:], in1=zcs[b][:, h * 2 + dc, :]), [('tqc', yb), ('zcs', b)], [('yoc', yb)])
                        r0 = 2 * CW + h * 256 + dc * P
                        S.dma('sp', lambda e, yb=yb, r0=r0, tb=tb: e.dma_start(
                            out=yown.ap()[r0:r0 + P, tb * TB:(tb + 1) * TB], in_=yo[yb][:]),
                            [('yoc', yb)], [('yown', r0, tb)], 'yoc%d' % yb)
        S.barrier()

    if stage >= 3 and MOBA:
        NYZ = 6
        oh_d = din("oh_in", [33, 640])
        relb_d = din("relb", [32, NH_A])
        negm_d = din("negm_in", [P, 512])
        E_d = din("E_in", [16, 16 * P])
        bv_d = dscr("s_bv", [NH_A, P, 640], F32)
        with ExitStack() as ph:
            oh = ph.enter_context(nc.sbuf_tensor("oh", [33, 640], F32))
            relb = ph.enter_context(nc.sbuf_tensor("relb_sb", [33, NH_A], F32))
            relbc = ph.enter_context(nc.sbuf_tensor("relbc", [33, P], F32))
            bvec = ph.enter_context(nc.sbuf_tensor("bvec", [P, 640], F32))
            Tst = ph.enter_context(nc.sbuf_tensor("Tst", [P, 256], F32))
            Tt = ph.enter_context(nc.sbuf_tensor("Tt", [P, NH_A, 3, 256], BF16))
            negm = ph.enter_context(nc.sbuf_tensor("negm", [P, 512], F32))
            Ef = ph.enter_context(nc.sbuf_tensor("Ef", [16, 16 * P], F32))
            Eb = ph.enter_context(nc.sbuf_tensor("Eb", [16, 16, P], BF16))
            c31 = ph.enter_context(nc.sbuf_tensor("c31", [P, NH_A], F32))
            kTh = [ph.enter_context(nc.sbuf_tensor("kTh%d" % i, [P, T], BF16)) for i in range(2)]
            qTh = [ph.enter_context(nc.sbuf_tensor("qTh%d" % i, [P, T], BF16)) for i in range(2)]
            zah = [ph.enter_context(nc.sbuf_tensor("zah%d" % i, [P, T], BF16)) for i in range(2)]
            vh = [ph.enter_context(nc.sbuf_tensor("vh%d" % i, [P, T // P, P], BF16)) for i in range(2)]
            km32 = ph.enter_context(nc.sbuf_tensor("km32", [P, 16], F32))
            kmb = ph.enter_context(nc.sbuf_tensor("kmb", [P, 16], BF16))
            gsb = ph.enter_context(nc.sbuf_tensor("gsb", [P, 512], F32))
            m8 = [ph.enter_context(nc.sbuf_tensor("m8_%d" % i, [P, 8], F32)) for i in range(2)]
            nm = [ph.enter_context(nc.sbuf_tensor("nm_%d" % i, [P, 16], BF16)) for i in range(2)]
            nmT = [ph.enter_context(nc.sbuf_tensor("nmT_%d" % i, [16, 256], BF16)) for i in range(2)]
            pTa = [ph.enter_context(nc.sbuf_tensor("pTa%d" % i, [P, 256], BF16)) for i in range(3)]
            reca = ph.enter_context(nc.sbuf_tensor("reca", [P, 256], F32))
            tqa = ph.enter_context(nc.sbuf_tensor("tqa", [P, 256], F32))
            yoa = [ph.enter_context(nc.sbuf_tensor("yoa%d" % i, [P, 256], BF16)) for i in range(2)]

            S.dma('sp', lambda e: e.dma_start(out=oh[:], in_=oh_d.ap()), [], ['oh'], 'a0')
            S.op('dve', lambda e: e.memset(relb[32:33, :], -30000.0), [], ['relb32'])
            S.dma('sp', lambda e: e.dma_start(out=relb[0:32, :], in_=relb_d.ap()), [], ['relb'], 'a1')
            S.dma('sp', lambda e: e.dma_start(out=negm[:], in_=negm_d.ap()), [], ['negm'], 'a2')
            S.dma('sp', lambda e: e.dma_start(out=Ef[:], in_=E_d.ap()), [], ['Ef'], 'a3')
            S.op('dve', lambda e: e.tensor_copy(out=Eb[:], in_=Ef[:].rearrange("p (n k) -> p n k", n=16)), ['Ef'], ['Eb'])
            S.dma('sp', lambda e: e.dma_start(out=c31[:], in_=relb_d.ap()[31:32, :].partition_broadcast(P)), [], ['c31'], 'a4')
            for h in range(NH_A):
                S.op('dve', lambda e, h=h: e.tensor_copy(out=relbc[:], in_=relb[:, h:h + 1].to_broadcast([33, P])),
                     ['relb', 'relb32'], ['relbc'])
                for half in range(2):
                    S.op('pe', lambda e, half=half: e.matmul(PS[half][:, 0:320], lhsT=relbc[:], rhs=oh[:, half * 320:(half + 1) * 320],
                                                            start=True, stop=True), ['relbc', 'oh'], [('ps', half)])
                    S.op('act', lambda e, half=half: e.activation(out=bvec[:, half * 320:(half + 1) * 320], in_=PS[half][:, 0:320],
                                                                  func=AF.Copy), [('ps', half)], ['bvec'])
                S.dma('sp', lambda e, h=h: e.dma_start(out=bv_d.ap()[h], in_=bvec[:]), ['bvec'], [('bv', h)], 'a5')
                for idx, delta in enumerate((0, -128, 128)):
                    src = bass.AP(bv_d, h * P * 640 + delta + 255, [[639, P], [1, 256]])
                    S.dma('sp', lambda e, src=src: e.dma_start(out=Tst[:], in_=src), [('bv', h)], ['Tst'], 'a6')
                    S.op('dve', lambda e, h=h, idx=idx: e.tensor_copy(out=Tt[:, h, idx, :], in_=Tst[:]), ['Tst'], ['Tt'])

            srot = 0
            for h in range(NH_A):
                hb = h % 2
                S.dma('sp', lambda e, h=h, hb=hb: e.dma_start(out=kTh[hb][:], in_=scr['ka'].ap()[h * P:(h + 1) * P, :]),
                      [], [('kTh', hb)], 'kTh%d' % hb)
                S.dma('sp', lambda e, h=h, hb=hb: e.dma_start(out=qTh[hb][:], in_=scr['qa'].ap()[h * P:(h + 1) * P, :]),
                      [], [('qTh', hb)], 'qTh%d' % hb)
                S.dma('sp', lambda e, h=h, hb=hb: e.dma_start(out=zah[hb][:], in_=scr['za'].ap()[h * P:(h + 1) * P, :]),
                      [], [('zah', hb)], 'zah%d' % hb)
                S.dma('sp', lambda e, h=h, hb=hb: e.dma_start(
                    out=vh[hb][:], in_=scr['va'].ap()[:, h * P:(h + 1) * P].rearrange("(t p) c -> p t c", p=P)),
                    [], [('vh', hb)], 'vh%d' % hb)
                S.op('dve', lambda e, hb=hb: e.tensor_reduce(out=km32[:], in_=kTh[hb][:].rearrange("p (n t) -> p n t", t=256),
                                                             axis=AX.X, op=ALU.add), [('kTh', hb)], ['km32'])
                S.op('dve', lambda e: e.tensor_scalar(out=kmb[:], in0=km32[:], scalar1=1.0 / 256, scalar2=None, op0=ALU.mult),
                     ['km32'], ['kmb'])
                for qt in range(T // P):
                    S.op('pe', lambda e, qt=qt, hb=hb: e.matmul(PS[0][:, qt * 16:(qt + 1) * 16], lhsT=qTh[hb][:, qt * P:(qt + 1) * P],
                                                                rhs=kmb[:], start=True, stop=True),
                         [('qTh', hb), 'kmb'], [('ps', 0)], sig=(qt == T // P - 1))
                S.op('dve', lambda e: e.tensor_add(out=gsb[:], in0=PS[0][:, :], in1=negm[:]), [('ps', 0), 'negm'], ['gsb'])
                for qb in range(T // 256):
                    nb = qb % 2
                    if qb > 0:
                        for qtl in range(2):
                            qt = 2 * qb + qtl
                            S.op('dve', lambda e, qt=qt, qtl=qtl: e.max(out=m8[qtl][:], in_=gsb[:, qt * 16:(qt + 1) * 16]),
                                 ['gsb'], [('m8', qtl)])
                            S.op('dve', lambda e, qt=qt, qtl=qtl: e.tensor_scalar(
                                out=nm[qtl][:], in0=gsb[:, qt * 16:(qt + 1) * 16], scalar1=m8[qtl][:, 2:3], scalar2=1.0,
                                op0=ALU.is_ge, op1=ALU.subtract), ['gsb', ('m8', qtl)], [('nm', qtl)])
                            S.op('pe', lambda e, qtl=qtl: e.transpose(
                                out=PS[1].bitcast(BF16)[0:16, qtl * P:(qtl + 1) * P], in_=nm[qtl][:], identity=ident[:]),
                                [('nm', qtl), 'ident'], [('ps', 1)])
                        S.op('act', lambda e, nb=nb: e.activation(out=nmT[nb][:], in_=PS[1].bitcast(BF16)[0:16, 0:256],
                                                                  func=AF.Copy), [('ps', 1)], [('nmT', nb)])
                    last = 2 * qb + 1
                    for kt in range(last + 1):
                        pi = 2 + srot % 3
                        pr = srot % 3
                        srot += 1
                        steps = [('qk', None)]
                        if kt < 2 * qb:
                            steps.append(('mask', kt // 2))
                        if kt >= 2 * qb - 1:
                            steps.append(('bias', {2 * qb: 0, 2 * qb + 1: 1, 2 * qb - 1: 2}[kt]))
                        for si_, (what, arg) in enumerate(steps):
                            st_, sp_ = (si_ == 0), (si_ == len(steps) - 1)
                            if what == 'qk':
                                S.op('pe', lambda e, pi=pi, kt=kt, qb=qb, hb=hb, st_=st_, sp_=sp_: e.matmul(
                                    PS[pi][:, 0:256], lhsT=kTh[hb][:, kt * P:(kt + 1) * P], rhs=qTh[hb][:, qb * 256:(qb + 1) * 256],
                                    start=st_, stop=sp_), [('kTh', hb), ('qTh', hb)], [('ps', pi)], sig=sp_)
                            elif what == 'mask':
                                S.op('pe', lambda e, pi=pi, arg=arg, nb=nb, st_=st_, sp_=sp_: e.matmul(
                                    PS[pi][:, 0:256], lhsT=Eb[:, arg, :], rhs=nmT[nb][:], start=st_, stop=sp_),
                                    ['Eb', ('nmT', nb)], [('ps', pi)], sig=sp_)
                            else:
                                S.op('pe', lambda e, pi=pi, arg=arg, h=h, st_=st_, sp_=sp_: e.matmul(
                                    PS[pi][:, 0:256], lhsT=ident[:], rhs=Tt[:, h, arg, :], start=st_, stop=sp_),
                                    ['ident', 'Tt'], [('ps', pi)], sig=sp_)
                        if kt <= 2 * qb - 2:
                            S.op('act', lambda e, pi=pi, pr=pr, h=h: e.activation(out=pTa[pr][:], in_=PS[pi][:, 0:256], func=AF.Exp,
                                                                              bias=c31[:, h:h + 1]), [('ps', pi), 'c31'], [('pTa', pr)])
                        else:
                            S.op('act', lambda e, pi=pi, pr=pr: e.activation(out=pTa[pr][:], in_=PS[pi][:, 0:256], func=AF.Exp),
                                 [('ps', pi)], [('pTa', pr)])
                        S.op('pe', lambda e, kt=kt, hb=hb, pr=pr, last=last: e.matmul(
                            PS[5][:, 0:256], lhsT=vh[hb][:, kt, :], rhs=pTa[pr][:], start=(kt == 0), stop=(kt == last)),
                            [('vh', hb), ('pTa', pr)], [('ps', 5)], sig=(kt == last))
                        S.op('pe', lambda e, kt=kt, pr=pr, last=last: e.matmul(
                            PS[6][:, 0:256], lhsT=ones_bf[:], rhs=pTa[pr][:], start=(kt == 0), stop=(kt == last)),
                            ['ones', ('pTa', pr)], [('ps', 6)], sig=True)
                    yb = qb % 2
                    S.op('dve', lambda e: e.reciprocal(out=reca[:], in_=PS[6][:, 0:256]), [('ps', 6)], ['reca'])
                    S.op('dve', lambda e: e.tensor_mul(out=tqa[:], in0=PS[5][:, 0:256], in1=reca[:]), [('ps', 5), 'reca'], ['tqa'])
                    S.op('pool', lambda e, yb=yb, hb=hb, qb=qb: e.tensor_mul(out=yoa[yb][:], in0=tqa[:],
                                                                             in1=zah[hb][:, qb * 256:(qb + 1) * 256]),
                         ['tqa', ('zah', hb)], [('yoa', yb)])
                    S.dma('sp', lambda e, yb=yb, h=h, qb=qb: e.dma_start(
                        out=yown.ap()[h * P:(h + 1) * P, qb * 256:(qb + 1) * 256], in_=yoa[yb][:]),
                        [('yoa', yb)], [('yown', h, qb)], 'yoa%d' % yb)
        S.barrier()

    if stage >= 3 and RWKV:
        NYZ = 0
        prm_d = din("prm", [P, 7 * 6])
        wd2_d = din("wd2", [96, CW])
        wa2_d = din("wa2", [96, CW])
        msk_d = din("msk_in", [P, 4 * P])
        CH = 128
        with ExitStack() as ph:
            def sb(name, shape, dt):
                return ph.enter_context(nc.sbuf_tensor(name, shape, dt))
            prm = sb("prm_sb", [P, 7, 6], F32)
            wd2 = sb("wd2_sb", [96, CW], BF16)
            wa2 = sb("wa2_sb", [96, CW], BF16)
            mskf = sb("mskf", [P, 4, P], F32)
            mskb = sb("mskb", [P, P], BF16)
            rr = sb("rr", [P, 6, CH], F32); kk_ = sb("kk_", [P, 6, CH], F32); vv = sb("vv", [P, 6, CH], F32)
            zz = sb("zz", [P, 6, CH], BF16)
            lwt = sb("lwt", [96, CH], F32); lat = sb("lat", [96, CH], F32)
            lwb = sb("lwb", [96, CH], BF16); lab = sb("lab", [96, CH], BF16)
            logw = sb("logw", [P, CH], F32); aa = sb("aa", [P, CH], F32); cs = sb("cs", [P, CH], F32)
            ep = sb("ep", [P, CH], F32); em = sb("em", [P, CH], F32); epv = sb("epv", [P, CH], F32)
            kkn = sb("kkn", [P, CH], F32); sqb = sb("sqb", [P, CH], BF16); rn = sb("rn", [P, CH], F32)
            kmod = sb("kmod", [P, CH], F32); tmpa = sb("tmpa", [P, CH], F32); bon = sb("bon", [P, CH], F32)
            ab = sb("ab", [P, CH], BF16); bt = sb("bt", [P, CH], BF16); ktl = sb("ktl", [P, CH], BF16); rbr = sb("rbr", [P, CH], BF16)
            vb16 = sb("vb16", [P, CH], BF16)
            btE = sb("btE", [P, P], BF16); btO = sb("btO", [P, P], BF16); ktE = sb("ktE", [P, P], BF16); ktO = sb("ktO", [P, P], BF16)
            vtok = sb("vtok", [P, P], BF16)
            wc = sb("wc", [P, 1], F32)
            Hm = sb("Hm", [P, 6, 64], F32); Hb = sb("Hb", [P, 6, 64], BF16)
            gAabT = sb("gAabT", [P, 2, P], BF16); gAab = sb("gAab", [P, 2, P], BF16)
            gAak = sb("gAak", [P, 2, P], BF16); gArb = sb("gArb", [P, 2, P], BF16); gArk = sb("gArk", [P, 2, P], BF16)
            Rm = sb("Rm", [P, 2, P], F32); Rb = sb("Rb", [P, 2, P], BF16)
            Qa = sb("Qa", [P, 2, P], BF16); QTa = sb("QTa", [P, 2, P], BF16)
            Qn = sb("Qn", [P, 2, P], BF16); QTn = sb("QTn", [P, 2, P], BF16)
            Xb = sb("Xb", [P, 2, 64], BF16); Ub = sb("Ub", [P, 2, 64], BF16)
            ytok = sb("ytok", [P, 2, 64], F32); yst = sb("yst", [P, 2, 4], F32); ynb = sb("ynb", [P, P], BF16)
            ysq = sb("ysq", [P, 2, 64], F32)
            yfm = sb("yfm", [P, CH], F32); yob = [sb("yob%d" % i, [P, CH], BF16) for i in range(2)]

            S.dma('sp', lambda e: e.dma_start(out=prm[:], in_=prm_d.ap().rearrange("p (a j) -> p a j", a=7)), [], ['prm'], 'b0')
            S.dma('pool', lambda e: e.dma_start(out=wd2[:], in_=wd2_d.ap()), [], ['wd2'], 'b1')
            S.dma('pool', lambda e: e.dma_start(out=wa2[:], in_=wa2_d.ap()), [], ['wa2'], 'b2')
            S.dma('sp', lambda e: e.dma_start(out=mskf[:], in_=msk_d.ap().rearrange("p (a t) -> p a t", a=4)), [], ['mskf'], 'b3')
            S.op('dve', lambda e: e.tensor_copy(out=mskb[:], in_=mskf[:, 3, :]), ['mskf'], ['mskb'])
            S.op('dve', lambda e: e.memset(Hm[:], 0.0), [], ['Hm'])
            S.op('dve', lambda e: e.memset(Hb[:], 0.0), [], ['Hb'])
            for nm_, tl in (('btE', btE), ('btO', btO), ('ktE', ktE), ('ktO', ktO)):
                S.op('pool', lambda e, tl=tl: e.memset(tl[:], 0.0), [], [nm_])
            EXPM05 = float(np.exp(-0.5))

            def D_(eng, fn, r, w):
                S.op(eng, fn, r, w)

            for c in range(T // CH):
                c0 = c * CH
                for (nm_, tl, src) in (('rr', rr, scr['rb']), ('kk_', kk_, scr['kb']), ('vv', vv, scr['vb']), ('zz', zz, scr['zb'])):
                    S.dma('sp', lambda e, tl=tl, src=src, c0=c0: e.dma_start(
                        out=tl[:], in_=src.ap()[:, c0:c0 + CH].rearrange("(j p) t -> p j t", p=P)), [], [nm_], 'b_' + nm_)
                S.dma('sp', lambda e, c0=c0: e.dma_start(out=lwt[:], in_=scr['lw'].ap()[:, c0:c0 + CH]), [], ['lwt'], 'b_lw')
                S.dma('sp', lambda e, c0=c0: e.dma_start(out=lat[:], in_=scr['la'].ap()[:, c0:c0 + CH]), [], ['lat'], 'b_la')
                D_('act', lambda e: e.activation(out=lwb[:], in_=lwt[:], func=AF.Tanh), ['lwt'], ['lwb'])
                D_('dve', lambda e: e.tensor_copy(out=lab[:], in_=lat[:]), ['lat'], ['lab'])
                for j in range(6):
                    D_('pe', lambda e, j=j: e.matmul(PS[0][:, 0:CH], lhsT=wd2[:, j * P:(j + 1) * P], rhs=lwb[:], start=True, stop=True),
                       ['wd2', 'lwb'], [('ps', 0)])
                    D_('pe', lambda e, j=j: e.matmul(PS[0][:, CH:2 * CH], lhsT=wa2[:, j * P:(j + 1) * P], rhs=lab[:], start=True, stop=True),
                       ['wa2', 'lab'], [('ps', 0)])
                    D_('act', lambda e, j=j: e.activation(out=logw[:], in_=PS[0][:, 0:CH], func=AF.Sigmoid, bias=prm[:, 0, j:j + 1]),
                       [('ps', 0), 'prm'], ['logw'])
                    D_('act', lambda e, j=j: e.activation(out=aa[:], in_=PS[0][:, CH:2 * CH], func=AF.Sigmoid, bias=prm[:, 1, j:j + 1]),
                       [('ps', 0), 'prm'], ['aa'])
                    D_('dve', lambda e: e.tensor_scalar(out=logw[:], in0=logw[:], scalar1=-EXPM05, scalar2=None, op0=ALU.mult),
                       ['logw'], ['logw'])
                    D_('dve', lambda e: e.tensor_tensor_scan(out=cs[:], data0=onesf[:, 0:CH], data1=logw[:], initial=0.0,
                                                             op0=ALU.mult, op1=ALU.add), ['logw', 'onesf'], ['cs'])
                    D_('act', lambda e: e.activation(out=ep[:], in_=cs[:], func=AF.Exp), ['cs'], ['ep'])
                    D_('act', lambda e: e.activation(out=em[:], in_=cs[:], func=AF.Exp, scale=-1.0), ['cs'], ['em'])
                    D_('dve', lambda e: e.tensor_sub(out=tmpa[:], in0=cs[:], in1=logw[:]), ['cs', 'logw'], ['tmpa'])
                    D_('act', lambda e: e.activation(out=epv[:], in_=tmpa[:], func=AF.Exp), ['tmpa'], ['epv'])
                    D_('dve', lambda e: e.tensor_copy(out=wc[:], in_=ep[:, CH - 1:CH]), ['ep'], ['wc'])
                    D_('dve', lambda e, j=j: e.tensor_scalar(out=kkn[:], in0=kk_[:, j, :], scalar1=prm[:, 2, j:j + 1], scalar2=None,
                                                            op0=ALU.mult), ['kk_', 'prm'], ['kkn'])
                    D_('dve', lambda e: e.tensor_mul(out=sqb[:], in0=kkn[:], in1=kkn[:]), ['kkn'], ['sqb'])
                    D_('pe', lambda e: e.matmul(PS[1][:, 0:CH], lhsT=mskb[:], rhs=sqb[:], start=True, stop=True), ['mskb', 'sqb'], [('ps', 1)])
                    D_('act', lambda e: e.activation(out=rn[:], in_=PS[1][:, 0:CH], func=AF.Sqrt), [('ps', 1)], ['rn'])
                    D_('dve', lambda e: e.tensor_scalar_max(out=rn[:], in0=rn[:], scalar1=1e-12), ['rn'], ['rn'])
                    D_('dve', lambda e: e.reciprocal(out=rn[:], in_=rn[:]), ['rn'], ['rn'])
                    D_('dve', lambda e: e.tensor_mul(out=kkn[:], in0=kkn[:], in1=rn[:]), ['kkn', 'rn'], ['kkn'])
                    D_('dve', lambda e, j=j: e.tensor_scalar(out=tmpa[:], in0=aa[:], scalar1=-1.0, scalar2=prm[:, 3, j:j + 1],
                                                            op0=ALU.add, op1=ALU.mult), ['aa', 'prm'], ['tmpa'])
                    D_('dve', lambda e, j=j: e.scalar_tensor_tensor(out=kmod[:], in0=tmpa[:], scalar=1.0, in1=kk_[:, j, :],
                                                                   op0=ALU.add, op1=ALU.mult), ['tmpa', 'kk_'], ['kmod'])
                    D_('dve', lambda e: e.scalar_tensor_tensor(out=ab[:], in0=kkn[:], scalar=-1.0, in1=epv[:], op0=ALU.mult, op1=ALU.mult),
                       ['kkn', 'epv'], ['ab'])
                    D_('dve', lambda e: e.tensor_mul(out=tmpa[:], in0=kkn[:], in1=aa[:]), ['kkn', 'aa'], ['tmpa'])
                    D_('dve', lambda e: e.tensor_mul(out=bt[:], in0=tmpa[:], in1=em[:]), ['tmpa', 'em'], ['bt'])
                    D_('dve', lambda e: e.tensor_mul(out=ktl[:], in0=kmod[:], in1=em[:]), ['kmod', 'em'], ['ktl'])
                    D_('dve', lambda e, j=j: e.tensor_mul(out=rbr[:], in0=rr[:, j, :], in1=ep[:]), ['rr', 'ep'], ['rbr'])
                    D_('dve', lambda e, j=j: e.tensor_copy(out=vb16[:], in_=vv[:, j, :]), ['vv'], ['vb16'])
                    D_('dve', lambda e, j=j: e.scalar_tensor_tensor(out=sqb[:], in0=rr[:, j, :], scalar=prm[:, 4, j:j + 1], in1=kmod[:],
                                                                   op0=ALU.mult, op1=ALU.mult), ['rr', 'kmod', 'prm', 'sqb'], ['sqb'])
                    D_('pe', lambda e: e.matmul(PS[1][:, CH:2 * CH], lhsT=mskb[:], rhs=sqb[:], start=True, stop=True), ['mskb', 'sqb'], [('ps', 1)])
                    D_('dve', lambda e, j=j: e.tensor_mul(out=bon[:], in0=PS[1][:, CH:2 * CH], in1=vv[:, j, :]), [('ps', 1), 'vv'], ['bon'])
                    pv1 = PS[1].bitcast(BF16)
                    D_('pe', lambda e, pv1=pv1: e.transpose(out=pv1[:, 512:640], in_=bt[:], identity=ident[:]), ['bt', 'ident'], [('ps', 1)])
                    D_('pe', lambda e, pv1=pv1: e.transpose(out=pv1[:, 640:768], in_=ktl[:], identity=ident[:]), ['ktl', 'ident'], [('ps', 1)])
                    D_('pe', lambda e, pv1=pv1: e.transpose(out=pv1[:, 768:896], in_=vb16[:], identity=ident[:]), ['vb16', 'ident'], [('ps', 1)])
                    D_('act', lambda e, pv1=pv1: e.activation(out=btE[:, 0:64], in_=pv1[:, 512:576], func=AF.Copy), [('ps', 1)], ['btE'])
                    D_('act', lambda e, pv1=pv1: e.activation(out=btO[:, 64:128], in_=pv1[:, 576:640], func=AF.Copy), [('ps', 1)], ['btO'])
                    D_('act', lambda e, pv1=pv1: e.activation(out=ktE[:, 0:64], in_=pv1[:, 640:704], func=AF.Copy), [('ps', 1)], ['ktE'])
                    D_('act', lambda e, pv1=pv1: e.activation(out=ktO[:, 64:128], in_=pv1[:, 704:768], func=AF.Copy), [('ps', 1)], ['ktO'])
                    D_('act', lambda e, pv1=pv1: e.activation(out=vtok[:], in_=pv1[:, 768:896], func=AF.Copy), [('ps', 1)], ['vtok'])
                    for h2 in range(2):
                        sl = slice(h2 * 64, (h2 + 1) * 64)
                        grams = ((bt, ab, 0), (ab, bt, 1), (ktl, ab, 2), (bt, rbr, 3))
                        for (L, R_, gi) in grams:
                            D_('pe', lambda e, L=L, R_=R_, gi=gi, sl=sl: e.matmul(PS[2][:, gi * P:(gi + 1) * P], lhsT=L[sl, :], rhs=R_[sl, :],
                                                                                 start=True, stop=True),
                               ['ab', 'bt', 'ktl', 'rbr'], [('ps', 2)])
                        D_('pe', lambda e, sl=sl: e.matmul(PS[3][:, 0:P], lhsT=ktl[sl, :], rhs=rbr[sl, :], start=True, stop=True),
                           ['ktl', 'rbr'], [('ps', 3)])
                        D_('dve', lambda e, h2=h2: e.tensor_mul(out=gAabT[:, h2, :], in0=PS[2][:, 0:P], in1=mskf[:, 0, :]), [('ps', 2), 'mskf'], ['gAabT'])
                        D_('dve', lambda e, h2=h2: e.tensor_mul(out=gAab[:, h2, :], in0=PS[2][:, P:2 * P], in1=mskf[:, 2, :]), [('ps', 2), 'mskf'], ['gAab'])
                        D_('dve', lambda e, h2=h2: e.tensor_mul(out=gAak[:, h2, :], in0=PS[2][:, 2 * P:3 * P], in1=mskf[:, 0, :]), [('ps', 2), 'mskf'], ['gAak'])
                        D_('dve', lambda e, h2=h2: e.tensor_mul(out=gArb[:, h2, :], in0=PS[2][:, 3 * P:4 * P], in1=mskf[:, 1, :]), [('ps', 2), 'mskf'], ['gArb'])
                        D_('dve', lambda e, h2=h2: e.tensor_mul(out=gArk[:, h2, :], in0=PS[3][:, 0:P], in1=mskf[:, 1, :]), [('ps', 3), 'mskf'], ['gArk'])
                    D_('dve', lambda e: e.tensor_mul(out=Rm[:], in0=gAabT[:], in1=mskf[:, 0:1, :].to_broadcast([P, 2, P])), ['gAabT', 'mskf'], ['Rm'])
                    for h2 in range(2):
                        D_('dve', lambda e, h2=h2: e.tensor_add(out=Rm[:, h2, :], in0=Rm[:, h2, :], in1=identf[:]), ['Rm', 'identf'], ['Rm'])
                    D_('dve', lambda e: e.tensor_copy(out=Rb[:], in_=Rm[:]), ['Rm'], ['Rb'])
                    Qc, QTc, Qx, QTx = gAabT, gAab, Qa, QTa
                    kq = {id(gAabT): 'gAabT', id(gAab): 'gAab', id(Qa): 'Qa', id(QTa): 'QTa', id(Qn): 'Qn', id(QTn): 'QTn'}
                    for lev in range(1, 7):
                        for h2 in range(2):
                            D_('pe', lambda e, h2=h2, Qc=Qc, QTc=QTc: e.matmul(PS[4][:, h2 * P:(h2 + 1) * P], lhsT=QTc[:, h2, :], rhs=Qc[:, h2, :],
                                                                             start=True, stop=True), [kq[id(Qc)], kq[id(QTc)]], [('ps', 4)])
                            D_('pe', lambda e, h2=h2, Qc=Qc, QTc=QTc: e.matmul(PS[4][:, (2 + h2) * P:(3 + h2) * P], lhsT=Qc[:, h2, :], rhs=QTc[:, h2, :],
                                                                             start=True, stop=True), [kq[id(Qc)], kq[id(QTc)]], [('ps', 4)])
                        D_('act', lambda e, Qx=Qx: e.activation(out=Qx[:], in_=PS[4][:, 0:2 * P].rearrange("p (a t) -> p a t", a=2), func=AF.Copy),
                           [('ps', 4)], [kq[id(Qx)]])
                        D_('act', lambda e, QTx=QTx: e.activation(out=QTx[:], in_=PS[4][:, 2 * P:4 * P].rearrange("p (a t) -> p a t", a=2), func=AF.Copy),
                           [('ps', 4)], [kq[id(QTx)]])
                        for h2 in range(2):
                            D_('pe', lambda e, h2=h2, QTx=QTx: e.matmul(PS[5][:, h2 * P:(h2 + 1) * P], lhsT=QTx[:, h2, :], rhs=Rb[:, h2, :],
                                                                      start=True, stop=True), [kq[id(QTx)], 'Rb'], [('ps', 5)])
                        D_('dve', lambda e: e.tensor_add(out=Rm[:], in0=Rm[:], in1=PS[5][:, 0:2 * P].rearrange("p (a t) -> p a t", a=2)),
                           [('ps', 5), 'Rm'], ['Rm'])
                        D_('dve', lambda e: e.tensor_copy(out=Rb[:], in_=Rm[:]), ['Rm'], ['Rb'])
                        Qc, QTc = Qx, QTx
                        Qx, QTx = (Qn, QTn) if Qx is Qa else (Qa, QTa)
                    for h2 in range(2):
                        sl = slice(h2 * 64, (h2 + 1) * 64)
                        vs = slice(h2 * 64, (h2 + 1) * 64)
                        D_('pe', lambda e, h2=h2, vs=vs: e.matmul(PS[6][:, h2 * 64:(h2 + 1) * 64], lhsT=gAak[:, h2, :], rhs=vtok[:, vs],
                                                                start=True, stop=False), ['gAak', 'vtok'], [('ps', 6)])
                        D_('pe', lambda e, h2=h2, sl=sl, j=j: e.matmul(PS[6][:, h2 * 64:(h2 + 1) * 64], lhsT=ab[sl, :], rhs=Hb[sl, j, :],
                                                                     start=False, stop=True), ['ab', 'Hb'], [('ps', 6)])
                    D_('act', lambda e: e.activation(out=Xb[:], in_=PS[6][:, 0:128].rearrange("p (a t) -> p a t", a=2), func=AF.Copy),
                       [('ps', 6)], ['Xb'])
                    for h2 in range(2):
                        D_('pe', lambda e, h2=h2: e.matmul(PS[6][:, 128 + h2 * 64:128 + (h2 + 1) * 64], lhsT=Rb[:, h2, :], rhs=Xb[:, h2, :],
                                                         start=True, stop=True), ['Rb', 'Xb'], [('ps', 6)])
                    D_('act', lambda e: e.activation(out=Ub[:], in_=PS[6][:, 128:256].rearrange("p (a t) -> p a t", a=2), func=AF.Copy),
                       [('ps', 6)], ['Ub'])
                    for h2 in range(2):
                        sl = slice(h2 * 64, (h2 + 1) * 64)
                        D_('pe', lambda e, h2=h2: e.matmul(PS[7][:, h2 * 64:(h2 + 1) * 64], lhsT=gArb[:, h2, :], rhs=Ub[:, h2, :],
                                                         start=True, stop=False), ['gArb', 'Ub'], [('ps', 7)])
                        D_('pe', lambda e, h2=h2, sl=sl: e.matmul(PS[7][:, h2 * 64:(h2 + 1) * 64], lhsT=gArk[:, h2, :], rhs=vtok[:, sl],
                                                                start=False, stop=False), ['gArk', 'vtok'], [('ps', 7)])
                        D_('pe', lambda e, h2=h2, sl=sl, j=j: e.matmul(PS[7][:, h2 * 64:(h2 + 1) * 64], lhsT=rbr[sl, :], rhs=Hb[sl, j, :],
                                                                     start=False, stop=True), ['rbr', 'Hb'], [('ps', 7)])
                    D_('pe', lambda e: e.matmul(PS[7][:, 128:192], lhsT=btE[:], rhs=Ub[:, 0, :], start=True, stop=False), ['btE', 'Ub'], [('ps', 7)])
                    D_('pe', lambda e: e.matmul(PS[7][:, 128:192], lhsT=btO[:], rhs=Ub[:, 1, :], start=False, stop=False), ['btO', 'Ub'], [('ps', 7)])
                    D_('pe', lambda e: e.matmul(PS[7][:, 128:192], lhsT=ktE[:], rhs=vtok[:, 0:64], start=False, stop=False), ['ktE', 'vtok'], [('ps', 7)])
                    D_('pe', lambda e: e.matmul(PS[7][:, 128:192], lhsT=ktO[:], rhs=vtok[:, 64:128], start=False, stop=True), ['ktO', 'vtok'], [('ps', 7)])
                    D_('dve', lambda e: e.tensor_copy(out=ytok[:], in_=PS[7][:, 0:128].rearrange("p (a t) -> p a t", a=2)), [('ps', 7)], ['ytok'])
                    D_('dve', lambda e, j=j: e.tensor_add(out=Hm[:, j, :], in0=Hm[:, j, :], in1=PS[7][:, 128:192]), [('ps', 7), 'Hm'], ['Hm'])
                    D_('dve', lambda e, j=j: e.tensor_scalar(out=Hm[:, j, :], in0=Hm[:, j, :], scalar1=wc[:, 0:1], scalar2=None, op0=ALU.mult),
                       ['Hm', 'wc'], ['Hm'])
                    D_('dve', lambda e, j=j: e.tensor_copy(out=Hb[:, j, :], in_=Hm[:, j, :]), ['Hm'], ['Hb'])
                    D_('dve', lambda e: e.tensor_reduce(out=yst[:, :, 0], in_=ytok[:], axis=AX.X, op=ALU.add), ['ytok'], ['yst'])
                    D_('dve', lambda e: e.tensor_scalar(out=yst[:, :, 0], in0=yst[:, :, 0], scalar1=1.0 / 64, scalar2=None, op0=ALU.mult), ['yst'], ['yst'])
                    D_('dve', lambda e: e.tensor_sub(out=ytok[:], in0=ytok[:], in1=yst[:, :, 0:1].to_broadcast([P, 2, 64])), ['ytok', 'yst'], ['ytok'])
                    D_('dve', lambda e: e.tensor_mul(out=ysq[:], in0=ytok[:], in1=ytok[:]), ['ytok'], ['ysq'])
                    D_('dve', lambda e: e.tensor_reduce(out=yst[:, :, 1], in_=ysq[:], axis=AX.X, op=ALU.add), ['ysq', 'yst'], ['yst'])
                    D_('dve', lambda e: e.tensor_scalar(out=yst[:, :, 1], in0=yst[:, :, 1], scalar1=1.0 / 64, scalar2=LNX_EPS, op0=ALU.mult, op1=ALU.add),
                       ['yst'], ['yst'])
                    D_('act', lambda e: e.activation(out=yst[:, :, 2], in_=yst[:, :, 1], func=AF.Sqrt), ['yst'], ['yst'])
                    D_('dve', lambda e: e.reciprocal(out=yst[:, :, 3], in_=yst[:, :, 2]), ['yst'], ['yst'])
                    D_('dve', lambda e: e.tensor_mul(out=ynb[:].rearrange("p (a t) -> p a t", a=2), in0=ytok[:],
                                                     in1=yst[:, :, 3:4].to_broadcast([P, 2, 64])), ['ytok', 'yst'], ['ynb'])
                    pv3 = PS[3].bitcast(BF16)
                    D_('pe', lambda e, pv3=pv3: e.transpose(out=pv3[:, 512:640], in_=ynb[:], identity=ident[:]), ['ynb', 'ident'], [('ps', 3)])
                    D_('dve', lambda e, j=j, pv3=pv3: e.tensor_scalar(out=yfm[:], in0=pv3[:, 512:640], scalar1=prm[:, 5, j:j + 1],
                                                                     scalar2=prm[:, 6, j:j + 1], op0=ALU.mult, op1=ALU.add),
                       [('ps', 3), 'prm'], ['yfm'])
                    D_('dve', lambda e: e.tensor_add(out=yfm[:], in0=yfm[:], in1=bon[:]), ['yfm', 'bon'], ['yfm'])
                    ob = (c * 6 + j) % 2
                    D_('pool', lambda e, ob=ob, j=j: e.tensor_mul(out=yob[ob][:], in0=yfm[:], in1=zz[:, j, :]), ['yfm', 'zz'], [('yob', ob)])
                    S.dma('sp', lambda e, ob=ob, j=j, c0=c0: e.dma_start(out=yown.ap()[CW + j * P:CW + (j + 1) * P, c0:c0 + CH], in_=yob[ob][:]),
                          [('yob', ob)], [('yownb', j, c)], 'yob%d' % ob)
        S.barrier()

    if stage >= 3:
        with ExitStack() as ph:
            zt = ph.enter_context(nc.sbuf_tensor("zt", [P, T], BF16))
            S.op('dve', lambda e: e.memset(zt[:], 0.0), [], ['zt'])
            for j in range(NYZ):
                j = j + (6 if NYZ == 6 else 0)
                S.dma('sp', lambda e, j=j: e.dma_start(out=yown.ap()[j * P:(j + 1) * P, :], in_=zt[:]),
                      ['zt'], [('yown', j)], 'yz')
        S.barrier()
        for j in range(D // P):
            S.dma('pool', lambda e, j=j: e.collective_compute(
                "AllGather", ALU.bypass, replica_groups=[[0, 1], [2, 3], [4, 5], [6, 7]],
                ins=[yown.ap()[j * P:(j + 1) * P, :].opt()], outs=[ygs[j].ap().opt()]), [], [('yg', j)], 'cc', inc=1)
        S.barrier()

    if stage >= 4:
        wpa_d = din("wpa", [1536, D])
        wpb_d = din("wpb", [1536, D])
        wpc_d = din("wpc", [1024, D])
        wo_d = din("wo", [D, D])
        xo_d = din("x_own", [T // 2, D])
        mT_d = dscr("s_mT", [D, T // 2], BF16)
        TH = T // 2
        chunks = []
        for j in range(12):
            chunks.append((wpa_d, j, (j // 6, (j % 6))))
        for j in range(12):
            chunks.append((wpb_d, j, (j // 6, 6 + (j % 6))))
        for j in range(8):
            chunks.append((wpc_d, j, (j // 4, 12 + (j % 4))))
        with ExitStack() as ph:
            ysel = ph.enter_context(nc.sbuf_tensor("ysel", [P, 32, TH], BF16))
            hv = ph.enter_context(nc.sbuf_tensor("hv", [P, 2], F32))
            t0 = [ph.enter_context(nc.sbuf_tensor("t0_%d" % i, [P, TH], BF16)) for i in range(2)]
            t1 = [ph.enter_context(nc.sbuf_tensor("t1_%d" % i, [P, TH], BF16)) for i in range(2)]
            wt = [ph.enter_context(nc.sbuf_tensor("wt%d" % i, [P, 32, P], BF16)) for i in range(2)]
            g0 = [ph.enter_context(nc.sbuf_tensor("g0_%d" % i, [P, 3, TB], BF16)) for i in range(2)]
            g1 = [ph.enter_context(nc.sbuf_tensor("g1_%d" % i, [P, 3, TB], BF16)) for i in range(2)]
            gs = [ph.enter_context(nc.sbuf_tensor("gs_%d" % i, [P, 3, TB], F32)) for i in range(2)]
            ma = [ph.enter_context(nc.sbuf_tensor("ma_%d" % i, [P, TB], F32)) for i in range(2)]
            mb = [ph.enter_context(nc.sbuf_tensor("mb_%d" % i, [P, TB], F32)) for i in range(2)]
            mo = [ph.enter_context(nc.sbuf_tensor("mo_%d" % i, [P, TB], BF16)) for i in range(2)]
            S.dma('sp', lambda e: e.dma_start(out=hv[:], in_=hhv_d.ap()), [], ['hv'], 'c3')
            for kc, (wd, wj, yrow) in enumerate(chunks):
                b = kc % 2
                rk, cj = yrow
                S.dma('sp', lambda e, b=b, rk=rk, cj=cj: e.dma_start(out=t0[b][:], in_=ygs[cj].ap()[rk * P:(rk + 1) * P, 0:TH]),
                      [], [('t0', b)], 't0%d' % b)
                S.dma('sp', lambda e, b=b, rk=rk, cj=cj: e.dma_start(out=t1[b][:], in_=ygs[cj].ap()[rk * P:(rk + 1) * P, TH:T]),
                      [], [('t1', b)], 't1%d' % b)
                S.op('dve', lambda e, b=b: e.tensor_scalar(out=t0[b][:], in0=t0[b][:], scalar1=hv[:, 0:1], scalar2=None,
                                                           op0=ALU.mult), [('t0', b), 'hv'], [('t0', b)])
                S.op('dve', lambda e, b=b, kc=kc: e.scalar_tensor_tensor(
                    out=ysel[:, kc, :], in0=t1[b][:], scalar=hv[:, 1:2], in1=t0[b][:], op0=ALU.mult, op1=ALU.add),
                    [('t0', b), ('t1', b), 'hv'], [('ysel', kc)])
            ysel_keys = [('ysel', kc) for kc in range(32)]
            it = 0
            for ct in range(D // P):
                wi = ct % 2
                for (wd, j0, n) in ((wpa_d, 0, 12), (wpb_d, 12, 12), (wpc_d, 24, 8)):
                    srcv = wd.ap()[:, ct * P:(ct + 1) * P].rearrange("(k p) c -> p k c", p=P)
                    S.dma('pool', lambda e, wi=wi, j0=j0, n=n, srcv=srcv: e.dma_start(out=wt[wi][:, j0:j0 + n, :], in_=srcv),
                          [], [('wt', wi)], 'wt%d_%d' % (wi, j0))
                for tb in range(TH // TB):
                    bi = it % 2
                    it += 1
                    pa, pb_, pc = PS[(it % 2) * 3 + 0], PS[(it % 2) * 3 + 1], PS[(it % 2) * 3 + 2]
                    pk = [('ps', (it % 2) * 3 + i) for i in range(3)]
                    for (pt, key, j0, n) in ((pa, pk[0], 0, 12), (pb_, pk[1], 12, 12), (pc, pk[2], 24, 8)):
                        for j in range(n):
                            S.op('pe', lambda e, pt=pt, j=j, j0=j0, n=n, tb=tb, wi=wi: e.matmul(
                                pt[:, :], lhsT=wt[wi][:, j0 + j, :], rhs=ysel[:, j0 + j, tb * TB:(tb + 1) * TB],
                                start=(j == 0), stop=(j == n - 1)),
                                [('wt', wi)] + (ysel_keys if j == 0 else []), [key], sig=(j == n - 1))
                    for i3 in range(3):
                        r0 = i3 * D + ct * P
                        S.dma('sp', lambda e, bi=bi, i3=i3, r0=r0, tb=tb: e.dma_start(
                            out=g0[bi][:, i3, :], in_=scr['g'].ap()[r0:r0 + P, tb * TB:(tb + 1) * TB]),
                            [], [('g0', bi)], 'g0%d' % bi)
                        S.dma('sp', lambda e, bi=bi, i3=i3, r0=r0, tb=tb: e.dma_start(
                            out=g1[bi][:, i3, :], in_=scr['g'].ap()[r0:r0 + P, TH + tb * TB:TH + (tb + 1) * TB]),
                            [], [('g1', bi)], 'g1%d' % bi)
                    S.op('pool', lambda e, bi=bi: e.tensor_scalar(out=gs[bi][:], in0=g0[bi][:], scalar1=hv[:, 0:1],
                                                                  scalar2=None, op0=ALU.mult),
                         [('g0', bi), 'hv'], [('gs', bi)])
                    S.op('dve', lambda e, bi=bi: e.scalar_tensor_tensor(
                        out=gs[bi][:], in0=g1[bi][:], scalar=hv[:, 1:2], in1=gs[bi][:], op0=ALU.mult, op1=ALU.add),
                        [('g1', bi), ('gs', bi), 'hv'], [('gs', bi)])
                    S.op('dve', lambda e, bi=bi, pa=pa: e.tensor_mul(out=ma[bi][:], in0=pa[:, :], in1=gs[bi][:, 0, :]),
                         [pk[0], ('gs', bi)], [('ma', bi)])
                    S.op('dve', lambda e, bi=bi, pb_=pb_: e.tensor_mul(out=mb[bi][:], in0=pb_[:, :], in1=gs[bi][:, 1, :]),
                         [pk[1], ('gs', bi)], [('mb', bi)])
                    S.op('pool', lambda e, bi=bi: e.tensor_add(out=ma[bi][:], in0=ma[bi][:], in1=mb[bi][:]),
                         [('ma', bi), ('mb', bi)], [('ma', bi)])
                    S.op('dve', lambda e, bi=bi, pc=pc: e.tensor_mul(out=mb[bi][:], in0=pc[:, :], in1=gs[bi][:, 2, :]),
                         [pk[2], ('gs', bi), ('ma', bi)], [('mb', bi)])
                    S.op('pool', lambda e, bi=bi: e.tensor_add(out=mo[bi][:], in0=ma[bi][:], in1=mb[bi][:]),
                         [('ma', bi), ('mb', bi)], [('mo', bi)])
                    S.dma('sp', lambda e, bi=bi, ct=ct, tb=tb: e.dma_start(
                        out=mT_d.ap()[ct * P:(ct + 1) * P, tb * TB:(tb + 1) * TB], in_=mo[bi][:]),
                        [('mo', bi)], [('mT', ct, tb)], 'mo%d' % bi)
        S.barrier()
        with ExitStack() as ph:
            mT = ph.enter_context(nc.sbuf_tensor("mT_sb", [P, 16, TH], BF16))
            wo = ph.enter_context(nc.sbuf_tensor("wo_sb", [P, 16, D], BF16))
            fgb = ph.enter_context(nc.sbuf_tensor("fgb", [P, D], F32))
            xo = [ph.enter_context(nc.sbuf_tensor("xo%d" % i, [P, D], F32)) for i in range(2)]
            ro = [ph.enter_context(nc.sbuf_tensor("ro%d" % i, [P, D], F32)) for i in range(2)]
            jk = ph.enter_context(nc.sbuf_tensor("jk", [P, D], BF16))
            s2 = [ph.enter_context(nc.sbuf_tensor("s2_%d" % i, [P, 2], F32)) for i in range(2)]
            S.dma('sp', lambda e: e.dma_start(out=fgb[:], in_=fng_d.ap().partition_broadcast(P)), [], ['fgb'], 'c4')
            for k in range(16):
                S.dma('sp', lambda e, k=k: e.dma_start(out=mT[:, k, :], in_=mT_d.ap()[k * P:(k + 1) * P, :]),
                      [], ['mTs'], 'mTs')
            for q in range(4):
                S.dma('pool', lambda e, q=q: e.dma_start(
                    out=wo[:, :, q * 512:(q + 1) * 512],
                    in_=wo_d.ap()[:, q * 512:(q + 1) * 512].rearrange("(k p) c -> p k c", p=P)),
                    [], ['wos'], 'wos')
            for tt in range(TH // P):
                b = tt % 2
                S.dma('sp', lambda e, b=b, tt=tt: e.dma_start(out=xo[b][:], in_=xo_d.ap()[tt * P:(tt + 1) * P, :]),
                      [], [('xo', b)], 'xo%d' % b)
                for n in range(4):
                    pi = (tt % 2) * 4 + n
                    for k in range(16):
                        S.op('pe', lambda e, pi=pi, k=k, n=n, tt=tt: e.matmul(
                            PS[pi][:, :], lhsT=mT[:, k, tt * P:(tt + 1) * P], rhs=wo[:, k, n * 512:(n + 1) * 512],
                            start=(k == 0), stop=(k == 15)), ['mTs', 'wos'] if k == 0 else [], [('ps', pi)], sig=(k == 15))
                    S.op('dve', lambda e, pi=pi, b=b, n=n: e.tensor_add(
                        out=ro[b][:, n * 512:(n + 1) * 512], in0=PS[pi][:, :], in1=xo[b][:, n * 512:(n + 1) * 512]),
                        [('ps', pi), ('xo', b)], [('ro', b)])
                S.op('act', lambda e, b=b: e.activation(out=jk[:], in_=ro[b][:], func=AF.Square, accum_out=s2[b][:, 0:1]),
                     [('ro', b)], [('s2', b), 'jk'])
                S.op('dve', lambda e, b=b: e.tensor_scalar(out=s2[b][:, 1:2], in0=s2[b][:, 0:1], scalar1=1.0 / D,
                                                           scalar2=RMS_EPS, op0=ALU.mult, op1=ALU.add),
                     [('s2', b)], [('s2', b)])
                S.op('act', lambda e, b=b: e.activation(out=s2[b][:, 1:2], in_=s2[b][:, 1:2], func=AF.Sqrt),
                     [('s2', b)], [('s2', b)])
                S.op('dve', lambda e, b=b: e.reciprocal(out=s2[b][:, 1:2], in_=s2[b][:, 1:2]), [('s2', b)], [('s2', b)])
                S.op('dve', lambda e, b=b: e.scalar_tensor_tensor(out=ro[b][:], in0=ro[b][:], scalar=s2[b][:, 1:2],
                                                                  in1=fgb[:], op0=ALU.mult, op1=ALU.mult),
                     [('ro', b), ('s2', b), 'fgb'], [('ro', b)])
                S.dma('sp', lambda e, b=b, tt=tt: e.dma_start(out=out_d.ap()[tt * P:(tt + 1) * P, :], in_=ro[b][:]),
                      [('ro', b)], [('out', tt)], 'out%d' % b)
    S.finish()
    S.emit()
    return nc, dbg_out


def _t5_bucket(d):
    import math
    n = max(int(d), 0)
    if n < 16:
        return n
    v = 16 + int(np.float32(np.log(np.float32(n) / np.float32(16)) / np.float32(math.log(128 / 16)) * np.float32(16)))
    return min(v, 31)


def _consts():
    oh = np.zeros((33, 640), np.float32)
    for j in range(640):
        d = j - 255
        if d < 0:
            oh[32, j] = 1.0
        else:
            oh[_t5_bucket(d), j] = 1.0
    negm = np.zeros((32, 16), np.float32)
    for qt in range(32):
        for n in range(16):
            if n >= qt // 2:
                negm[qt, n] = -1e30
    negm = np.ascontiguousarray(np.broadcast_to(negm.reshape(1, 512), (128, 512)))
    E = np.zeros((16, 16, 128), np.float32)
    for n in range(16):
        E[n, n, :] = 30000.0
    return oh, negm, E.reshape(16, 16 * 128)


_OH, _NEGM, _EIN = _consts()
_pp = np.arange(128)[:, None]
_ff = np.arange(128)[None, :]
_MSK = np.ascontiguousarray(np.concatenate([(_pp < _ff), (_pp <= _ff), (_pp > _ff), ((_pp // 64) == (_ff // 64))], 1).astype(np.float32))


def _tile_cols(v, ntile):
    return np.ascontiguousarray(np.asarray(v, np.float32).reshape(ntile, 128).T)


def make_in_maps(inp):
    maps = []
    w_in = inp['w_in'][0]
    offs = np.cumsum([0, 3 * 1536, 1536, 3 * 1536, 1536, 96, 96, 1024, 1024, 3 * 2048])
    o_qkv_a, o_z_a, o_rkv_b, o_z_b, o_lw, o_la, o_q_c, o_z_c, o_g = offs[:9]
    ident = np.eye(128, dtype=np.float32)
    for c in range(8):
        b, hh = c // 2, c % 2
        a0 = hh * CW
        cc0 = hh * CC
        cols = np.concatenate([
            np.arange(o_qkv_a + a0, o_qkv_a + a0 + CW),
            np.arange(o_qkv_a + 1536 + a0, o_qkv_a + 1536 + a0 + CW),
            np.arange(o_qkv_a + 3072 + a0, o_qkv_a + 3072 + a0 + CW),
            np.arange(o_z_a + a0, o_z_a + a0 + CW),
            np.arange(o_rkv_b + a0, o_rkv_b + a0 + CW),
            np.arange(o_rkv_b + 1536 + a0, o_rkv_b + 1536 + a0 + CW),
            np.arange(o_rkv_b + 3072 + a0, o_rkv_b + 3072 + a0 + CW),
            np.arange(o_z_b + a0, o_z_b + a0 + CW),
            np.arange(o_lw, o_lw + 96), np.arange(o_la, o_la + 96),
            np.arange(o_q_c + cc0, o_q_c + cc0 + CC),
            np.arange(o_z_c + cc0, o_z_c + cc0 + CC)])
        m = {}
        m['x'] = np.ascontiguousarray(inp['x'][b])
        m['mem'] = np.ascontiguousarray(inp['mem'][b])
        m['w1'] = np.ascontiguousarray(w_in[:, cols])
        m['wg'] = np.ascontiguousarray(w_in[:, o_g:o_g + 3 * D])
        m['norm_g'] = np.ascontiguousarray(inp['norm_g'][0][None, :])
        m['mem_norm_g'] = np.ascontiguousarray(inp['mem_norm_g'][0][None, :])
        m['final_norm_g'] = np.ascontiguousarray(inp['final_norm_g'][None, :])
        m["ident_in"] = ident
        m["ones_in"] = np.ones((128, 128), np.float32)
        m["oh_in"] = _OH
        prm = np.zeros((128, 7, 6), np.float32)
        for ai, nm_ in enumerate(('rw_w0', 'rw_a0', 'rw_k_k', 'rw_k_a', 'rw_r_k', 'rw_lnx_w', 'rw_lnx_b')):
            vfull = np.asarray(inp[nm_][0], np.float32).reshape(-1)
            prm[:, ai, :] = _tile_cols(vfull[a0:a0 + CW], 6)
        m["prm"] = prm.reshape(128, 42)
        m["wd2"] = np.ascontiguousarray(inp['rw_w_decay2'][0][:, a0:a0 + CW])
        m["wa2"] = np.ascontiguousarray(inp['rw_w_aaa2'][0][:, a0:a0 + CW])
        m["msk_in"] = _MSK
        m["negm_in"] = _NEGM
        m["E_in"] = _EIN
        m["relb"] = np.ascontiguousarray(inp['rel_bias'][:, hh * NH_A:(hh + 1) * NH_A])
        wm = inp['w_mem_kv'][0]
        m['wmkv'] = np.ascontiguousarray(np.concatenate([wm[:, cc0:cc0 + CC], wm[:, 1024 + cc0:1024 + cc0 + CC]], 1))
        hv = np.zeros((128, 2), np.float32); hv[:, 0] = 1 - hh; hv[:, 1] = hh
        m['hhv'] = hv
        m['wpa'] = np.ascontiguousarray(inp['w_proj_a'][0])
        m['wpb'] = np.ascontiguousarray(inp['w_proj_b'][0])
        m['wpc'] = np.ascontiguousarray(inp['w_proj_c'][0])
        m['wo'] = np.ascontiguousarray(inp['w_out'][0])
        m['x_own'] = np.ascontiguousarray(inp['x'][b][hh * (T // 2):(hh + 1) * (T // 2)])
        mu1 = np.zeros((128, 20), np.float32)
        mu1[:, 0:6] = _tile_cols(inp['rw_mu_r'][0][a0:a0 + CW], 6)
        mu1[:, 6:12] = _tile_cols(inp['rw_mu_k'][0][a0:a0 + CW], 6)
        mu1[:, 12:18] = _tile_cols(inp['rw_mu_v'][0][a0:a0 + CW], 6)
        mu1[:96, 18] = inp['rw_mu_w'][0]
        mu1[:96, 19] = inp['rw_mu_a'][0]
        m['mu1'] = mu1
        maps.append(m)
    return maps


_NC_CACHE = {}


def kernel(**inputs):
    if 'nc' not in _NC_CACHE:
        import os
        _NC_CACHE['nc'] = build(stage=int(os.environ.get('K_STAGE', '99')), dbg=False)[0]
    nc = _NC_CACHE['nc']
    maps = make_in_maps(inputs)
    used = set(t.name for t in nc.m.functions[0].allocs) if False else set(t for t in ('x', 'mem', 'w1', 'wg', 'norm_g', 'mem_norm_g', 'final_norm_g', 'ident_in', 'hhv', 'mu1',
                           'wpa', 'wpb', 'wpc', 'wo', 'x_own', 'ones_in', 'wmkv', 'oh_in', 'negm_in', 'E_in', 'relb', 'prm', 'wd2', 'wa2', 'msk_in'))
    maps = [{k: v for k, v in m.items() if k in used} for m in maps]
    res = run_bass_kernel_spmd(nc, maps, core_ids=list(range(8)))
    out = np.zeros((4, T, D), np.float32)
    for c in range(8):
        b, hh = c // 2, c % 2
        out[b, hh * (T // 2):(hh + 1) * (T // 2)] = np.asarray(res.results[c]['out'], np.float32)
    return out
```

```python
import numpy as np
import ml_dtypes
import concourse.bass as bass
import concourse.mybir as mybir
from concourse.bass_utils import run_bass_kernel_spmd
from contextlib import ExitStack

F32 = mybir.dt.float32
BF16 = mybir.dt.bfloat16
AF = mybir.ActivationFunctionType
ALU = mybir.AluOpType
AX = mybir.AxisListType

ENG = ('pe', 'act', 'dve', 'pool', 'sp')
BLOCKNAME = {'pe': 'tensor', 'act': 'scalar', 'dve': 'vector', 'pool': 'gpsimd', 'sp': 'sync'}
EPOCH = 30000


class Sched:
    def __init__(self, nc):
        self.nc = nc
        self.ops = {e: [] for e in ENG}
        self.nsig = {e: 0 for e in ENG}
        self.res = {}
        self.waited = {e: {} for e in ENG}
        self.dma_cnt = {}
        self.semkeys = {}

    def _ev_next(self, eng):
        n = self.nsig[eng] + 1
        ep = (n - 1) // EPOCH
        return ((eng, ep), n - ep * EPOCH)

    def _deps(self, eng, reads, writes):
        waits = {}

        def need(k, v):
            if waits.get(k, 0) < v:
                waits[k] = v
        for key in reads:
            r = self.res.get(key)
            if r and r[0] is not None:
                need(*r[0])
        for key in writes:
            r = self.res.get(key)
            if r:
                if r[0] is not None:
                    need(*r[0])
                for k, v in r[1].items():
                    need(k, v)
        out = []
        nk, nv = self._ev_next(eng)
        for k, v in waits.items():
            if k == nk and v >= nv:
                continue
            if self.waited[eng].get(k, 0) < v:
                self.waited[eng][k] = v
                out.append((k, v))
        return out

    def _record(self, ev, reads, writes):
        for key in reads:
            r = self.res.setdefault(key, [None, {}])
            if r[1].get(ev[0], 0) < ev[1]:
                r[1][ev[0]] = ev[1]
        for key in writes:
            self.res[key] = [ev, {}]

    def op(self, eng, fn, reads=(), writes=(), sig=True):
        waits = self._deps(eng, reads, writes)
        ev = self._ev_next(eng)
        if sig:
            self.nsig[eng] += 1
            self.semkeys[ev[0]] = 1
            self.ops[eng].append((fn, waits, ev[0], 1))
        else:
            self.ops[eng].append((fn, waits, None, 0))
        self._record(ev, reads, writes)

    def dma(self, q, fn, reads, writes, stream, inc=16):
        waits = self._deps(q, reads, writes)
        semkey = ('dma', stream)
        c = self.dma_cnt.get(semkey, 0) + inc
        self.dma_cnt[semkey] = c
        self.semkeys[semkey] = 1
        self.ops[q].append((fn, waits, semkey, inc))
        self._record((semkey, c), reads, writes)

    def barrier(self):
        evs = []
        for e in ENG:
            n = self.nsig[e]
            if n == 0:
                continue
            ep = (n - 1) // EPOCH
            evs.append(((e, ep), n - ep * EPOCH))
        for k, c in self.dma_cnt.items():
            evs.append((k, c))
        for e in ENG:
            waits = []
            for k, v in evs:
                if k[0] == e:
                    continue
                if self.waited[e].get(k, 0) < v:
                    self.waited[e][k] = v
                    waits.append((k, v))
            self.ops[e].append((None, waits, None, 0))
        self.res = {}

    def finish(self):
        waits = []
        for k, c in self.dma_cnt.items():
            if self.waited['sp'].get(k, 0) < c:
                waits.append((k, c))
        self.ops['sp'].append((None, waits, None, 0))

    def emit(self):
        nc = self.nc
        with ExitStack() as st:
            sems = {}
            for i, k in enumerate(self.semkeys):
                sems[k] = st.enter_context(nc.semaphore("s%d" % i))
            block = st.enter_context(nc.Block())
            for eng in ENG:
                def body(e, eng=eng):
                    for fn, waits, semkey, inc in self.ops[eng]:
                        for (k, v) in waits:
                            e.wait_ge(sems[k], v)
                        if fn is None:
                            continue
                        ins = fn(e)
                        if semkey is not None:
                            ins.then_inc(sems[semkey], inc)
                getattr(block, BLOCKNAME[eng])(body)


D = 2048
T = 4096
NH_A = 6
NH_B = 12
NH_C = 2
CW = 768
CC = 512
MEM = 256
NC1 = 8 * CW + 192 + 2 * CC
TB = 512
NTB = T // TB
RMS_EPS = 1e-6
LNX_EPS = 64e-5

SEGS = []
_o = 0
for _n, _w, _k in (('qa', CW, 'q'), ('ka', CW, 'k'), ('va', CW, 'v'), ('za', CW, 'silu'),
                   ('rb', CW, 'shift'), ('kb', CW, 'shift'), ('vb', CW, 'shift'), ('zb', CW, 'silu'),
                   ('lw', 96, 'shift'), ('la', 96, 'shift'), ('qc', CC, 'q'), ('zc', CC, 'silu')):
    SEGS.append((_n, _o, _w, _k))
    _o += _w
assert _o == NC1


def build(stage=99, dbg=False):
    MOBA = True
    RWKV = True
    import os
    BIS = int(os.environ.get('BIS', '9'))
    nc = bass.Bass("TRN2", target_bir_lowering=False)
    S = Sched(nc)
    P = 128

    def din(name, shape, dt=F32):
        return nc.dram_tensor(name, list(shape), dt, kind="ExternalInput")

    def dscr(name, shape, dt):
        if dbg:
            return nc.dram_tensor(name, list(shape), dt, kind="ExternalOutput")
        return nc.dram_tensor(name, list(shape), dt)

    x_d = din("x", [T, D])
    mem_d = din("mem", [MEM, D])
    w1_d = din("w1", [D, NC1])
    wg_d = din("wg", [D, 3 * D])
    ng_d = din("norm_g", [1, D])
    mng_d = din("mem_norm_g", [1, D])
    fng_d = din("final_norm_g", [1, D])
    ident_d = din("ident_in", [P, P], F32)
    out_d = nc.dram_tensor("out", [T // 2, D], F32, kind="ExternalOutput")

    scr = {}
    scr['qa'] = dscr("s_qa", [CW, T], BF16)
    scr['ka'] = dscr("s_ka", [CW, T], BF16)
    scr['va'] = dscr("s_va", [T, CW], BF16)
    scr['za'] = dscr("s_za", [CW, T], BF16)
    scr['rb'] = dscr("s_rb", [CW, T], F32)
    scr['kb'] = dscr("s_kb", [CW, T], F32)
    scr['vb'] = dscr("s_vb", [CW, T], F32)
    scr['zb'] = dscr("s_zb", [CW, T], BF16)
    scr['lw'] = dscr("s_lw", [96, T], F32)
    scr['la'] = dscr("s_la", [96, T], F32)
    scr['qc'] = dscr("s_qc", [CC, T], BF16)
    scr['zc'] = dscr("s_zc", [CC, T], BF16)
    scr['g'] = dscr("s_g", [3 * D, T], BF16)
    hhv_d = din("hhv", [128, 2], F32)

    wmkv_d = din("wmkv", [D, 1024])
    kmT_d = dscr("s_kmT", [CC, MEM], BF16)
    vm_d = dscr("s_vm", [MEM, CC], BF16)
    ones_d = din("ones_in", [P, P], F32)
    dbg_out = {}

    PS = [nc.alloc_psum_tensor("ps%d" % i, [P, 512], F32) for i in range(8)]

    ident = nc.alloc_sbuf_tensor("ident", [P, P], BF16)
    identf = nc.alloc_sbuf_tensor("identf", [P, P], F32)
    S.dma('sp', lambda e: e.dma_start(out=identf[:], in_=ident_d.ap()), [], ['identf'], 'c0')
    S.op('dve', lambda e: e.tensor_copy(out=ident[:], in_=identf[:]), ['identf'], ['ident'])

    with ExitStack() as ph:
        hT = ph.enter_context(nc.sbuf_tensor("hT", [P, 16, T], BF16))
        gbc = ph.enter_context(nc.sbuf_tensor("gbc", [P, D], F32))
        xt = [ph.enter_context(nc.sbuf_tensor("xt%d" % i, [P, D], F32)) for i in range(2)]
        xn = [ph.enter_context(nc.sbuf_tensor("xn%d" % i, [P, D], BF16)) for i in range(2)]
        junk = ph.enter_context(nc.sbuf_tensor("junk", [P, D], BF16))
        ss = [ph.enter_context(nc.sbuf_tensor("ss%d" % i, [P, 2], F32)) for i in range(2)]

        S.dma('sp', lambda e: e.dma_start(out=gbc[:], in_=ng_d.ap().partition_broadcast(P)), [], ['gbc'], 'c1')

        def norm_T(src_ap, ntile, dst, gkey, tag):
            for tt in range(ntile):
                b = tt % 2
                S.dma('sp', lambda e, tt=tt, b=b: e.dma_start(out=xt[b][:], in_=src_ap[tt * P:(tt + 1) * P, :]),
                      [], [('xt', b)], 'xt%d' % b)
                S.op('act', lambda e, b=b: e.activation(out=junk[:], in_=xt[b][:], func=AF.Square,
                                                        accum_out=ss[b][:, 0:1]),
                     [('xt', b)], [('ss', b), 'junk'])
                if BIS < 2:
                    continue
                S.op('dve', lambda e, b=b: e.tensor_scalar(out=ss[b][:, 1:2], in0=ss[b][:, 0:1], scalar1=1.0 / D,
                                                           scalar2=RMS_EPS, op0=ALU.mult, op1=ALU.add),
                     [('ss', b)], [('ss', b)])
                S.op('act', lambda e, b=b: e.activation(out=ss[b][:, 1:2], in_=ss[b][:, 1:2], func=AF.Sqrt),
                     [('ss', b)], [('ss', b)])
                S.op('dve', lambda e, b=b: e.reciprocal(out=ss[b][:, 1:2], in_=ss[b][:, 1:2]),
                     [('ss', b)], [('ss', b)])
                S.op('dve', lambda e, b=b: e.scalar_tensor_tensor(out=xn[b][:], in0=xt[b][:], scalar=ss[b][:, 1:2],
                                                                  in1=gbc[:], op0=ALU.mult, op1=ALU.mult),
                     [('xt', b), ('ss', b), gkey], [('xn', b)])
                for half in range(2 if BIS >= 3 else 0):
                    pb = PS[(tt * 2 + half) % 4]
                    pbv = pb.bitcast(BF16)
                    for k8 in range(8):
                        k = half * 8 + k8
                        S.op('pe', lambda e, b=b, k=k, k8=k8, pbv=pbv: e.transpose(
                            out=pbv[:, k8 * P:(k8 + 1) * P], in_=xn[b][:, k * P:(k + 1) * P], identity=ident[:]),
                            [('xn', b), 'ident'], [('ps', (tt * 2 + half) % 4)], sig=(k8 == 7))
                    eng = 'act' if half == 0 else 'pool_no'
                    eng = 'act' if half == 0 else 'dve'
                    src = pbv[:, 0:1024].rearrange("p (k t) -> p k t", k=8)
                    dstv = dst[:, half * 8:(half + 1) * 8, tt * P:(tt + 1) * P]
                    if eng == 'act':
                        S.op('act', lambda e, src=src, dstv=dstv: e.activation(out=dstv, in_=src, func=AF.Copy),
                             [('ps', (tt * 2 + half) % 4)], [(tag, tt)])
                    else:
                        S.op('dve', lambda e, src=src, dstv=dstv: e.tensor_copy(out=dstv, in_=src),
                             [('ps', (tt * 2 + half) % 4)], [(tag, tt)])

        norm_T(x_d.ap(), int(os.environ.get('NT', T // P)), hT, 'gbc', 'hT')

        if dbg:
            dbg_out['hT'] = nc.dram_tensor("dbg_hT", [P, 16, T], BF16, kind="ExternalOutput")
            S.dma('sp', lambda e: e.dma_start(out=dbg_out['hT'].ap(), in_=hT[:]),
                  [('hT', tt) for tt in range(T // P)], ['dbg_hT'], 'dbg')

        WG = 256
        NWB = 2
        wbuf = [ph.enter_context(nc.sbuf_tensor("wb%d" % i, [P, 16, WG], BF16)) for i in range(NWB)]
        NST = 4
        stg32 = [ph.enter_context(nc.sbuf_tensor("st32_%d" % i, [P, TB + 1], F32)) for i in range(NST)]
        stg16 = [ph.enter_context(nc.sbuf_tensor("st16_%d" % i, [P, TB], BF16)) for i in range(NST)]
        tmp32 = [ph.enter_context(nc.sbuf_tensor("tm32_%d" % i, [P, TB], F32)) for i in range(2)]
        mu_sb = ph.enter_context(nc.sbuf_tensor("mu_sb", [P, 20], F32))
        mu_d = din("mu1", [P, 20])
        S.dma('sp', lambda e: e.dma_start(out=mu_sb[:], in_=mu_d.ap()), [], ['mu'], 'c2')

        wcnt = [0]
        scnt = [0]
        pcnt = [0]

        def load_w(src_d, c0, cw):
            i = wcnt[0] % NWB
            wcnt[0] += 1
            srcv = src_d.ap()[:, c0:c0 + cw].rearrange("(k p) c -> p k c", p=P)
            S.dma('pool', lambda e: e.dma_start(out=wbuf[i][:, :, 0:cw], in_=srcv), [], [('wb', i)], 'wb%d' % i)
            return i

        def proj_group(src_d, c0, cw, kind, name, seg_off, tbs, mucol0):
            wi = load_w(src_d, c0, cw)
            if kind == 'v':
                for tb in tbs:
                    for t4 in range(4):
                        tt = tb * 4 + t4
                        pi = 4 + pcnt[0] % 4
                        pcnt[0] += 1
                        for k in range(16):
                            S.op('pe', lambda e, k=k, tt=tt, pi=pi: e.matmul(
                                PS[pi][:, 0:cw], lhsT=hT[:, k, tt * P:(tt + 1) * P], rhs=wbuf[wi][:, k, 0:cw],
                                start=(k == 0), stop=(k == 15)),
                                [('wb', wi)], [('ps', pi)], sig=(k == 15))
                        si = scnt[0] % NST
                        scnt[0] += 1
                        S.op('act', lambda e, pi=pi, si=si: e.activation(out=stg16[si][:, 0:cw], in_=PS[pi][:, 0:cw],
                                                                         func=AF.Copy),
                             [('ps', pi)], [('s16', si)])
                        S.dma('sp', lambda e, si=si, tt=tt: e.dma_start(
                            out=scr[name].ap()[tt * P:(tt + 1) * P, seg_off:seg_off + cw], in_=stg16[si][:, 0:cw]),
                            [('s16', si)], [(name, 'v', tt, seg_off)], 'st%d' % si)
                return
            nsub = (cw + P - 1) // P
            for sub in range(nsub):
                m = min(P, cw - sub * P)
                row0 = seg_off + sub * P
                for tb in tbs:
                    pi = 4 + pcnt[0] % 4
                    pcnt[0] += 1
                    for k in range(16):
                        S.op('pe', lambda e, k=k, tb=tb, pi=pi, sub=sub, m=m: e.matmul(
                            PS[pi][0:m, :], lhsT=wbuf[wi][:, k, sub * P:sub * P + m], rhs=hT[:, k, tb * TB:(tb + 1) * TB],
                            start=(k == 0), stop=(k == 15)),
                            [('wb', wi)], [('ps', pi)], sig=(k == 15))
                    si = scnt[0] % NST
                    scnt[0] += 1
                    if kind in ('q', 'k', 'silu', 'sig'):
                        if kind == 'q':
                            sc = 128 ** -0.5 if name == 'qa' else 256 ** -0.5
                            f = lambda e, pi=pi, si=si, m=m, sc=sc: e.activation(
                                out=stg16[si][0:m, :], in_=PS[pi][0:m, :], func=AF.Copy, scale=sc)
                        elif kind == 'k':
                            f = lambda e, pi=pi, si=si, m=m: e.activation(
                                out=stg16[si][0:m, :], in_=PS[pi][0:m, :], func=AF.Copy)
                        elif kind == 'silu':
                            f = lambda e, pi=pi, si=si, m=m: e.activation(
                                out=stg16[si][0:m, :], in_=PS[pi][0:m, :], func=AF.Silu)
                        else:
                            f = lambda e, pi=pi, si=si, m=m: e.activation(
                                out=stg16[si][0:m, :], in_=PS[pi][0:m, :], func=AF.Sigmoid)
                        S.op('act', f, [('ps', pi)], [('s16', si)])
                        tcol = tb * TB
                        S.dma('sp', lambda e, si=si, m=m, row0=row0, tcol=tcol: e.dma_start(
                            out=scr[name].ap()[row0:row0 + m, tcol:tcol + TB], in_=stg16[si][0:m, :]),
                            [('s16', si)], [(name, row0, tcol)], 'st%d' % si)
                    else:
                        mucol = mucol0 + sub
                        first = (tb == tbs[0])
                        prev_si = (scnt[0] - 2) % NST
                        if first:
                            S.op('dve', lambda e, si=si, m=m: e.memset(stg32[si][0:m, 0:1], 0.0), [], [('s32', si)])
                        else:
                            S.op('dve', lambda e, si=si, m=m, ps_=prev_si: e.tensor_copy(
                                out=stg32[si][0:m, 0:1], in_=stg32[ps_][0:m, TB:TB + 1]),
                                [('s32c', prev_si)], [('s32', si)])
                        S.op('act', lambda e, pi=pi, si=si, m=m: e.activation(
                            out=stg32[si][0:m, 1:TB + 1], in_=PS[pi][0:m, :], func=AF.Copy),
                            [('ps', pi)], [('s32', si), ('s32c', si)])
                        tb2 = si % 2
                        S.op('dve', lambda e, si=si, m=m, tb2=tb2: e.tensor_sub(
                            out=tmp32[tb2][0:m, :], in0=stg32[si][0:m, 0:TB], in1=stg32[si][0:m, 1:TB + 1]),
                            [('s32', si), ('s32c', si)], [('t32', tb2)])
                        S.op('dve', lambda e, si=si, m=m, tb2=tb2, mucol=mucol: e.scalar_tensor_tensor(
                            out=tmp32[tb2][0:m, :], in0=tmp32[tb2][0:m, :], scalar=mu_sb[0:m, mucol:mucol + 1],
                            in1=stg32[si][0:m, 1:TB + 1], op0=ALU.mult, op1=ALU.add),
                            [('t32', tb2), ('s32', si), ('s32c', si), 'mu'], [('t32', tb2)])
                        S.dma('sp', lambda e, m=m, row0=row0, tb=tb, tb2=tb2: e.dma_start(
                            out=scr[name].ap()[row0:row0 + m, tb * TB:(tb + 1) * TB], in_=tmp32[tb2][0:m, :]),
                            [('t32', tb2)], [(name, row0, tb * TB)], 'tm%d' % tb2)

        if stage >= 1:
            memT = ph.enter_context(nc.sbuf_tensor("memT", [P, 16, MEM], BF16))
            S.dma('sp', lambda e: e.dma_start(out=gbc[:], in_=mng_d.ap().partition_broadcast(P)), [], ['gbc'], 'c1')
            norm_T(mem_d.ap(), MEM // P, memT, 'gbc', 'memT')
            memkeys = [('memT', tt) for tt in range(MEM // P)]
            for c0 in (0, 256):
                wi = load_w(wmkv_d, c0, 256)
                for sub in range(2):
                    pi = 4 + pcnt[0] % 4
                    pcnt[0] += 1
                    for k in range(16):
                        S.op('pe', lambda e, k=k, pi=pi, sub=sub, wi=wi: e.matmul(
                            PS[pi][:, 0:MEM], lhsT=wbuf[wi][:, k, sub * P:(sub + 1) * P], rhs=memT[:, k, :],
                            start=(k == 0), stop=(k == 15)), [('wb', wi)] + memkeys, [('ps', pi)], sig=(k == 15))
                    si = scnt[0] % NST
                    scnt[0] += 1
                    S.op('act', lambda e, pi=pi, si=si: e.activation(out=stg16[si][:, 0:MEM], in_=PS[pi][:, 0:MEM],
                                                                     func=AF.Copy), [('ps', pi)], [('s16', si)])
                    S.dma('sp', lambda e, si=si, r0=c0 + sub * P: e.dma_start(
                        out=kmT_d.ap()[r0:r0 + P, :], in_=stg16[si][:, 0:MEM]), [('s16', si)], [('kmT', c0, sub)],
                        'st%d' % si)
            for c0 in (512, 768):
                wi = load_w(wmkv_d, c0, 256)
                for mt in range(2):
                    pi = 4 + pcnt[0] % 4
                    pcnt[0] += 1
                    for k in range(16):
                        S.op('pe', lambda e, k=k, pi=pi, mt=mt, wi=wi: e.matmul(
                            PS[pi][:, 0:256], lhsT=memT[:, k, mt * P:(mt + 1) * P], rhs=wbuf[wi][:, k, 0:256],
                            start=(k == 0), stop=(k == 15)), [('wb', wi)] + memkeys, [('ps', pi)], sig=(k == 15))
                    si = scnt[0] % NST
                    scnt[0] += 1
                    S.op('act', lambda e, pi=pi, si=si: e.activation(out=stg16[si][:, 0:256], in_=PS[pi][:, 0:256],
                                                                     func=AF.Copy), [('ps', pi)], [('s16', si)])
                    S.dma('sp', lambda e, si=si, mt=mt, c0=c0: e.dma_start(
                        out=vm_d.ap()[mt * P:(mt + 1) * P, c0 - 512:c0 - 256], in_=stg16[si][:, 0:256]),
                        [('s16', si)], [('vm', c0, mt)], 'st%d' % si)

        mucols = {'rb': 0, 'kb': 6, 'vb': 12, 'lw': 18, 'la': 19}
        for (name, off, width, kind) in SEGS:
            if stage < 1:
                break
            c = 0
            while c < width:
                cw = min(WG, width - c)
                proj_group(w1_d, off + c, cw, kind, name, c, list(range(NTB)), mucols.get(name, 0) + c // P)
                c += cw
        if stage >= 2:
            c = 0
            while c < 3 * D:
                proj_group(wg_d, c, WG, 'k' if False else 'sig', 'g', c, list(range(NTB)), 0)
                c += WG

    S.barrier()

    yown = dscr("s_yown", [D, T], BF16)
    ygs = [nc.dram_tensor("s_yg%d" % j, [2 * P, T], BF16) for j in range(D // P)]
    ones_bf = nc.alloc_sbuf_tensor("ones_bf", [P, P], BF16)
    onesf = nc.alloc_sbuf_tensor("onesf", [P, P], F32)
    S.dma('sp', lambda e: e.dma_start(out=onesf[:], in_=ones_d.ap()), [], ['onesf'], 'c0')
    S.op('dve', lambda e: e.tensor_copy(out=ones_bf[:], in_=onesf[:]), ['onesf'], ['ones'])
    NYZ = 12

    if stage >= 3:
        with ExitStack() as ph:
            kmT = ph.enter_context(nc.sbuf_tensor("kmT_sb", [P, 4, MEM], BF16))
            vm = ph.enter_context(nc.sbuf_tensor("vm_sb", [P, 2, CC], BF16))
            qcs = [ph.enter_context(nc.sbuf_tensor("qcs%d" % i, [P, 4, TB], BF16)) for i in range(2)]
            zcs = [ph.enter_context(nc.sbuf_tensor("zcs%d" % i, [P, 4, TB], BF16)) for i in range(2)]
            pT = [ph.enter_context(nc.sbuf_tensor("pTc%d" % i, [P, 2, TB], BF16)) for i in range(2)]
            rec = [ph.enter_context(nc.sbuf_tensor("recc%d" % i, [P, TB], F32)) for i in range(2)]
            tq = [ph.enter_context(nc.sbuf_tensor("tqc%d" % i, [P, TB], F32)) for i in range(2)]
            yo = [ph.enter_context(nc.sbuf_tensor("yoc%d" % i, [P, TB], BF16)) for i in range(2)]
            S.dma('sp', lambda e: e.dma_start(out=kmT[:], in_=kmT_d.ap().rearrange("(c p) m -> p c m", p=P)),
                  [], ['kmTs'], 'c5')
            S.dma('sp', lambda e: e.dma_start(out=vm[:], in_=vm_d.ap().rearrange("(t p) c -> p t c", p=P)),
                  [], ['vms'], 'c6')
            itc = 0
            for tb in range(NTB):
                b = tb % 2
                S.dma('sp', lambda e, b=b, tb=tb: e.dma_start(
                    out=qcs[b][:], in_=scr['qc'].ap()[:, tb * TB:(tb + 1) * TB].rearrange("(c p) t -> p c t", p=P)),
                    [], [('qcs', b)], 'qcs%d' % b)
                S.dma('sp', lambda e, b=b, tb=tb: e.dma_start(
                    out=zcs[b][:], in_=scr['zc'].ap()[:, tb * TB:(tb + 1) * TB].rearrange("(c p) t -> p c t", p=P)),
                    [], [('zcs', b)], 'zcs%d' % b)
                for h in range(NH_C):
                    pb = itc % 2
                    itc += 1
                    for mt in range(2):
                        for dc in range(2):
                            S.op('pe', lambda e, mt=mt, dc=dc, h=h, b=b: e.matmul(
                                PS[mt][:, :], lhsT=kmT[:, h * 2 + dc, mt * P:(mt + 1) * P], rhs=qcs[b][:, h * 2 + dc, :],
                                start=(dc == 0), stop=(dc == 1)), ['kmTs', ('qcs', b)], [('ps', mt)], sig=(dc == 1))
                        S.op('act', lambda e, mt=mt, pb=pb: e.activation(out=pT[pb][:, mt, :], in_=PS[mt][:, :],
                                                                         func=AF.Exp), [('ps', mt)], [('pTc', pb, mt)])
                    for mt in range(2):
                        S.op('pe', lambda e, mt=mt, pb=pb: e.matmul(
                            PS[4][:, :], lhsT=ones_bf[:], rhs=pT[pb][:, mt, :], start=(mt == 0), stop=(mt == 1)),
                            ['ones', ('pTc', pb, 0), ('pTc', pb, 1)], [('ps', 4)], sig=(mt == 1))
                    S.op('dve', lambda e, pb=pb: e.reciprocal(out=rec[pb][:], in_=PS[4][:, :]), [('ps', 4)], [('recc', pb)])
                    for dc in range(2):
                        for mt in range(2):
                            S.op('pe', lambda e, mt=mt, dc=dc, h=h, pb=pb: e.matmul(
                                PS[2 + dc][:, :], lhsT=vm[:, mt, h * 256 + dc * P:h * 256 + (dc + 1) * P],
                                rhs=pT[pb][:, mt, :], start=(mt == 0), stop=(mt == 1)),
                                ['vms', ('pTc', pb, 0), ('pTc', pb, 1)], [('ps', 2 + dc)], sig=(mt == 1))
                        yb = (itc * 2 + dc) % 2
                        S.op('dve', lambda e, dc=dc, pb=pb, yb=yb: e.tensor_mul(
                            out=tq[yb][:], in0=PS[2 + dc][:, :], in1=rec[pb][:]), [('ps', 2 + dc), ('recc', pb)], [('tqc', yb)])
                        S.op('pool', lambda e, dc=dc, h=h, b=b, yb=yb: e.tensor_mul(
                            out=yo[yb][:], in0=tq[yb][:], in1=zcs[b][:, h * 2 + dc, :]), [('tqc', yb), ('zcs', b)], [('yoc', yb)])
                        r0 = 2 * CW + h * 256 + dc * P
                        S.dma('sp', lambda e, yb=yb, r0=r0, tb=tb: e.dma_start(
                            out=yown.ap()[r0:r0 + P, tb * TB:(tb + 1) * TB], in_=yo[yb][:]),
                            [('yoc', yb)], [('yown', r0, tb)], 'yoc%d' % yb)
        S.barrier()

    if stage >= 3 and MOBA:
        NYZ = 6
        oh_d = din("oh_in", [33, 640])
        relb_d = din("relb", [32, NH_A])
        negm_d = din("negm_in", [P, 512])
        E_d = din("E_in", [16, 16 * P])
        bv_d = dscr("s_bv", [NH_A, P, 640], F32)
        with ExitStack() as ph:
            oh = ph.enter_context(nc.sbuf_tensor("oh", [33, 640], F32))
            relb = ph.enter_context(nc.sbuf_tensor("relb_sb", [33, NH_A], F32))
            relbc = ph.enter_context(nc.sbuf_tensor("relbc", [33, P], F32))
            bvec = ph.enter_context(nc.sbuf_tensor("bvec", [P, 640], F32))
            Tst = ph.enter_context(nc.sbuf_tensor("Tst", [P, 256], F32))
            Tt = ph.enter_context(nc.sbuf_tensor("Tt", [P, NH_A, 3, 256], BF16))
            negm = ph.enter_context(nc.sbuf_tensor("negm", [P, 512], F32))
            Ef = ph.enter_context(nc.sbuf_tensor("Ef", [16, 16 * P], F32))
            Eb = ph.enter_context(nc.sbuf_tensor("Eb", [16, 16, P], BF16))
            c31 = ph.enter_context(nc.sbuf_tensor("c31", [P, NH_A], F32))
            kTh = [ph.enter_context(nc.sbuf_tensor("kTh%d" % i, [P, T], BF16)) for i in range(2)]
            qTh = [ph.enter_context(nc.sbuf_tensor("qTh%d" % i, [P, T], BF16)) for i in range(2)]
            zah = [ph.enter_context(nc.sbuf_tensor("zah%d" % i, [P, T], BF16)) for i in range(2)]
            vh = [ph.enter_context(nc.sbuf_tensor("vh%d" % i, [P, T // P, P], BF16)) for i in range(2)]
            km32 = ph.enter_context(nc.sbuf_tensor("km32", [P, 16], F32))
            kmb = ph.enter_context(nc.sbuf_tensor("kmb", [P, 16], BF16))
            gsb = ph.enter_context(nc.sbuf_tensor("gsb", [P, 512], F32))
            m8 = [ph.enter_context(nc.sbuf_tensor("m8_%d" % i, [P, 8], F32)) for i in range(2)]
            nm = [ph.enter_context(nc.sbuf_tensor("nm_%d" % i, [P, 16], BF16)) for i in range(2)]
            nmT = [ph.enter_context(nc.sbuf_tensor("nmT_%d" % i, [16, 256], BF16)) for i in range(2)]
            pTa = [ph.enter_context(nc.sbuf_tensor("pTa%d" % i, [P, 256], BF16)) for i in range(3)]
            reca = ph.enter_context(nc.sbuf_tensor("reca", [P, 256], F32))
            tqa = ph.enter_context(nc.sbuf_tensor("tqa", [P, 256], F32))
            yoa = [ph.enter_context(nc.sbuf_tensor("yoa%d" % i, [P, 256], BF16)) for i in range(2)]

            S.dma('sp', lambda e: e.dma_start(out=oh[:], in_=oh_d.ap()), [], ['oh'], 'a0')
            S.op('dve', lambda e: e.memset(relb[32:33, :], -30000.0), [], ['relb32'])
            S.dma('sp', lambda e: e.dma_start(out=relb[0:32, :], in_=relb_d.ap()), [], ['relb'], 'a1')
            S.dma('sp', lambda e: e.dma_start(out=negm[:], in_=negm_d.ap()), [], ['negm'], 'a2')
            S.dma('sp', lambda e: e.dma_start(out=Ef[:], in_=E_d.ap()), [], ['Ef'], 'a3')
            S.op('dve', lambda e: e.tensor_copy(out=Eb[:], in_=Ef[:].rearrange("p (n k) -> p n k", n=16)), ['Ef'], ['Eb'])
            S.dma('sp', lambda e: e.dma_start(out=c31[:], in_=relb_d.ap()[31:32, :].partition_broadcast(P)), [], ['c31'], 'a4')
            for h in range(NH_A):
                S.op('dve', lambda e, h=h: e.tensor_copy(out=relbc[:], in_=relb[:, h:h + 1].to_broadcast([33, P])),
                     ['relb', 'relb32'], ['relbc'])
                for half in range(2):
                    S.op('pe', lambda e, half=half: e.matmul(PS[half][:, 0:320], lhsT=relbc[:], rhs=oh[:, half * 320:(half + 1) * 320],
                                                            start=True, stop=True), ['relbc', 'oh'], [('ps', half)])
                    S.op('act', lambda e, half=half: e.activation(out=bvec[:, half * 320:(half + 1) * 320], in_=PS[half][:, 0:320],
                                                                  func=AF.Copy), [('ps', half)], ['bvec'])
                S.dma('sp', lambda e, h=h: e.dma_start(out=bv_d.ap()[h], in_=bvec[:]), ['bvec'], [('bv', h)], 'a5')
                for idx, delta in enumerate((0, -128, 128)):
                    src = bass.AP(bv_d, h * P * 640 + delta + 255, [[639, P], [1, 256]])
                    S.dma('sp', lambda e, src=src: e.dma_start(out=Tst[:], in_=src), [('bv', h)], ['Tst'], 'a6')
                    S.op('dve', lambda e, h=h, idx=idx: e.tensor_copy(out=Tt[:, h, idx, :], in_=Tst[:]), ['Tst'], ['Tt'])

            srot = 0
            for h in range(NH_A):
                hb = h % 2
                S.dma('sp', lambda e, h=h, hb=hb: e.dma_start(out=kTh[hb][:], in_=scr['ka'].ap()[h * P:(h + 1) * P, :]),
                      [], [('kTh', hb)], 'kTh%d' % hb)
                S.dma('sp', lambda e, h=h, hb=hb: e.dma_start(out=qTh[hb][:], in_=scr['qa'].ap()[h * P:(h + 1) * P, :]),
                      [], [('qTh', hb)], 'qTh%d' % hb)
                S.dma('sp', lambda e, h=h, hb=hb: e.dma_start(out=zah[hb][:], in_=scr['za'].ap()[h * P:(h + 1) * P, :]),
                      [], [('zah', hb)], 'zah%d' % hb)
                S.dma('sp', lambda e, h=h, hb=hb: e.dma_start(
                    out=vh[hb][:], in_=scr['va'].ap()[:, h * P:(h + 1) * P].rearrange("(t p) c -> p t c", p=P)),
                    [], [('vh', hb)], 'vh%d' % hb)
                S.op('dve', lambda e, hb=hb: e.tensor_reduce(out=km32[:], in_=kTh[hb][:].rearrange("p (n t) -> p n t", t=256),
                                                             axis=AX.X, op=ALU.add), [('kTh', hb)], ['km32'])
                S.op('dve', lambda e: e.tensor_scalar(out=kmb[:], in0=km32[:], scalar1=1.0 / 256, scalar2=None, op0=ALU.mult),
                     ['km32'], ['kmb'])
                for qt in range(T // P):
                    S.op('pe', lambda e, qt=qt, hb=hb: e.matmul(PS[0][:, qt * 16:(qt + 1) * 16], lhsT=qTh[hb][:, qt * P:(qt + 1) * P],
                                                                rhs=kmb[:], start=True, stop=True),
                         [('qTh', hb), 'kmb'], [('ps', 0)], sig=(qt == T // P - 1))
                S.op('dve', lambda e: e.tensor_add(out=gsb[:], in0=PS[0][:, :], in1=negm[:]), [('ps', 0), 'negm'], ['gsb'])
                for qb in range(T // 256):
                    nb = qb % 2
                    if qb > 0:
                        for qtl in range(2):
                            qt = 2 * qb + qtl
                            S.op('dve', lambda e, qt=qt, qtl=qtl: e.max(out=m8[qtl][:], in_=gsb[:, qt * 16:(qt + 1) * 16]),
                                 ['gsb'], [('m8', qtl)])
                            S.op('dve', lambda e, qt=qt, qtl=qtl: e.tensor_scalar(
                                out=nm[qtl][:], in0=gsb[:, qt * 16:(qt + 1) * 16], scalar1=m8[qtl][:, 2:3], scalar2=1.0,
                                op0=ALU.is_ge, op1=ALU.subtract), ['gsb', ('m8', qtl)], [('nm', qtl)])
                            S.op('pe', lambda e, qtl=qtl: e.transpose(
                                out=PS[1].bitcast(BF16)[0:16, qtl * P:(qtl + 1) * P], in_=nm[qtl][:], identity=ident[:]),
                                [('nm', qtl), 'ident'], [('ps', 1)])
                        S.op('act', lambda e, nb=nb: e.activation(out=nmT[nb][:], in_=PS[1].bitcast(BF16)[0:16, 0:256],
                                                                  func=AF.Copy), [('ps', 1)], [('nmT', nb)])
                    last = 2 * qb + 1
                    for kt in range(last + 1):
                        pi = 2 + srot % 3
                        pr = srot % 3
                        srot += 1
                        steps = [('qk', None)]
                        if kt < 2 * qb:
                            steps.append(('mask', kt // 2))
                        if kt >= 2 * qb - 1:
                            steps.append(('bias', {2 * qb: 0, 2 * qb + 1: 1, 2 * qb - 1: 2}[kt]))
                        for si_, (what, arg) in enumerate(steps):
                            st_, sp_ = (si_ == 0), (si_ == len(steps) - 1)
                            if what == 'qk':
                                S.op('pe', lambda e, pi=pi, kt=kt, qb=qb, hb=hb, st_=st_, sp_=sp_: e.matmul(
                                    PS[pi][:, 0:256], lhsT=kTh[hb][:, kt * P:(kt + 1) * P], rhs=qTh[hb][:, qb * 256:(qb + 1) * 256],
                                    start=st_, stop=sp_), [('kTh', hb), ('qTh', hb)], [('ps', pi)], sig=sp_)
                            elif what == 'mask':
                                S.op('pe', lambda e, pi=pi, arg=arg, nb=nb, st_=st_, sp_=sp_: e.matmul(
                                    PS[pi][:, 0:256], lhsT=Eb[:, arg, :], rhs=nmT[nb][:], start=st_, stop=sp_),
                                    ['Eb', ('nmT', nb)], [('ps', pi)], sig=sp_)
                            else:
                                S.op('pe', lambda e, pi=pi, arg=arg, h=h, st_=st_, sp_=sp_: e.matmul(
                                    PS[pi][:, 0:256], lhsT=ident[:], rhs=Tt[:, h, arg, :], start=st_, stop=sp_),
                                    ['ident', 'Tt'], [('ps', pi)], sig=sp_)
                        if kt <= 2 * qb - 2:
                            S.op('act', lambda e, pi=pi, pr=pr, h=h: e.activation(out=pTa[pr][:], in_=PS[pi][:, 0:256], func=AF.Exp,
                                                                              bias=c31[:, h:h + 1]), [('ps', pi), 'c31'], [('pTa', pr)])
                        else:
                            S.op('act', lambda e, pi=pi, pr=pr: e.activation(out=pTa[pr][:], in_=PS[pi][:, 0:256], func=AF.Exp),
                                 [('ps', pi)], [('pTa', pr)])
                        S.op('pe', lambda e, kt=kt, hb=hb, pr=pr, last=last: e.matmul(
                            PS[5][:, 0:256], lhsT=vh[hb][:, kt, :], rhs=pTa[pr][:], start=(kt == 0), stop=(kt == last)),
                            [('vh', hb), ('pTa', pr)], [('ps', 5)], sig=(kt == last))
                        S.op('pe', lambda e, kt=kt, pr=pr, last=last: e.matmul(
                            PS[6][:, 0:256], lhsT=ones_bf[:], rhs=pTa[pr][:], start=(kt == 0), stop=(kt == last)),
                            ['ones', ('pTa', pr)], [('ps', 6)], sig=True)
                    yb = qb % 2
                    S.op('dve', lambda e: e.reciprocal(out=reca[:], in_=PS[6][:, 0:256]), [('ps', 6)], ['reca'])
                    S.op('dve', lambda e: e.tensor_mul(out=tqa[:], in0=PS[5][:, 0:256], in1=reca[:]), [('ps', 5), 'reca'], ['tqa'])
                    S.op('pool', lambda e, yb=yb, hb=hb, qb=qb: e.tensor_mul(out=yoa[yb][:], in0=tqa[:],
                                                                             in1=zah[hb][:, qb * 256:(qb + 1) * 256]),
                         ['tqa', ('zah', hb)], [('yoa', yb)])
                    S.dma('sp', lambda e, yb=yb, h=h, qb=qb: e.dma_start(
                        out=yown.ap()[h * P:(h + 1) * P, qb * 256:(qb + 1) * 256], in_=yoa[yb][:]),
                        [('yoa', yb)], [('yown', h, qb)], 'yoa%d' % yb)
        S.barrier()

    if stage >= 3 and RWKV:
        NYZ = 0
        prm_d = din("prm", [P, 7 * 6])
        wd2_d = din("wd2", [96, CW])
        wa2_d = din("wa2", [96, CW])
        msk_d = din("msk_in", [P, 4 * P])
        CH = 128
        with ExitStack() as ph:
            def sb(name, shape, dt):
                return ph.enter_context(nc.sbuf_tensor(name, shape, dt))
            prm = sb("prm_sb", [P, 7, 6], F32)
            wd2 = sb("wd2_sb", [96, CW], BF16)
            wa2 = sb("wa2_sb", [96, CW], BF16)
            mskf = sb("mskf", [P, 4, P], F32)
            mskb = sb("mskb", [P, P], BF16)
            rr = sb("rr", [P, 6, CH], F32); kk_ = sb("kk_", [P, 6, CH], F32); vv = sb("vv", [P, 6, CH], F32)
            zz = sb("zz", [P, 6, CH], BF16)
            lwt = sb("lwt", [96, CH], F32); lat = sb("lat", [96, CH], F32)
            lwb = sb("lwb", [96, CH], BF16); lab = sb("lab", [96, CH], BF16)
            logw_2 = [sb("logw_%d" % i_, [P, CH], F32) for i_ in range(2)]; aa_2 = [sb("aa_%d" % i_, [P, CH], F32) for i_ in range(2)]; cs_2 = [sb("cs_%d" % i_, [P, CH], F32) for i_ in range(2)]
            ep_2 = [sb("ep_%d" % i_, [P, CH], F32) for i_ in range(2)]; em_2 = [sb("em_%d" % i_, [P, CH], F32) for i_ in range(2)]; epv_2 = [sb("epv_%d" % i_, [P, CH], F32) for i_ in range(2)]
            kkn_2 = [sb("kkn_%d" % i_, [P, CH], F32) for i_ in range(2)]; sqb_2 = [sb("sqb_%d" % i_, [P, CH], BF16) for i_ in range(2)]; rn_2 = [sb("rn_%d" % i_, [P, CH], F32) for i_ in range(2)]
            kmod_2 = [sb("kmod_%d" % i_, [P, CH], F32) for i_ in range(2)]; tmpa_2 = [sb("tmpa_%d" % i_, [P, CH], F32) for i_ in range(2)]; bon_2 = [sb("bon_%d" % i_, [P, CH], F32) for i_ in range(2)]
            ab_2 = [sb("ab_%d" % i_, [P, CH], BF16) for i_ in range(2)]; bt_2 = [sb("bt_%d" % i_, [P, CH], BF16) for i_ in range(2)]; ktl_2 = [sb("ktl_%d" % i_, [P, CH], BF16) for i_ in range(2)]; rbr_2 = [sb("rbr_%d" % i_, [P, CH], BF16) for i_ in range(2)]
            vb16_2 = [sb("vb16_%d" % i_, [P, CH], BF16) for i_ in range(2)]
            btE_2 = [sb("btE_%d" % i_, [P, P], BF16) for i_ in range(2)]; btO_2 = [sb("btO_%d" % i_, [P, P], BF16) for i_ in range(2)]; ktE_2 = [sb("ktE_%d" % i_, [P, P], BF16) for i_ in range(2)]; ktO_2 = [sb("ktO_%d" % i_, [P, P], BF16) for i_ in range(2)]
            vtok_2 = [sb("vtok_%d" % i_, [P, P], BF16) for i_ in range(2)]
            wc_2 = [sb("wc_%d" % i_, [P, 1], F32) for i_ in range(2)]
            Hm = sb("Hm", [P, 6, 64], F32); Hb = sb("Hb", [P, 6, 64], BF16)
            gAabT_2 = [sb("gAabT_%d" % i_, [P, 2, P], BF16) for i_ in range(2)]; gAab_2 = [sb("gAab_%d" % i_, [P, 2, P], BF16) for i_ in range(2)]
            gAak_2 = [sb("gAak_%d" % i_, [P, 2, P], BF16) for i_ in range(2)]; gArb_2 = [sb("gArb_%d" % i_, [P, 2, P], BF16) for i_ in range(2)]; gArk_2 = [sb("gArk_%d" % i_, [P, 2, P], BF16) for i_ in range(2)]
            Rm_2 = [sb("Rm_%d" % i_, [P, 2, P], F32) for i_ in range(2)]; Rb_2 = [sb("Rb_%d" % i_, [P, 2, P], BF16) for i_ in range(2)]
            Qa_2 = [sb("Qa_%d" % i_, [P, 2, P], BF16) for i_ in range(2)]; QTa_2 = [sb("QTa_%d" % i_, [P, 2, P], BF16) for i_ in range(2)]
            Qn_2 = [sb("Qn_%d" % i_, [P, 2, P], BF16) for i_ in range(2)]; QTn_2 = [sb("QTn_%d" % i_, [P, 2, P], BF16) for i_ in range(2)]
            Xb_2 = [sb("Xb_%d" % i_, [P, 2, 64], BF16) for i_ in range(2)]; Ub_2 = [sb("Ub_%d" % i_, [P, 2, 64], BF16) for i_ in range(2)]
            ytok_2 = [sb("ytok_%d" % i_, [P, 2, 64], F32) for i_ in range(2)]; yst_2 = [sb("yst_%d" % i_, [P, 2, 4], F32) for i_ in range(2)]; ynb_2 = [sb("ynb_%d" % i_, [P, P], BF16) for i_ in range(2)]
            ysq_2 = [sb("ysq_%d" % i_, [P, 2, 64], F32) for i_ in range(2)]
            yfm_2 = [sb("yfm_%d" % i_, [P, CH], F32) for i_ in range(2)]; yob = [sb("yob%d" % i, [P, CH], BF16) for i in range(2)]

            S.dma('sp', lambda e: e.dma_start(out=prm[:], in_=prm_d.ap().rearrange("p (a j) -> p a j", a=7)), [], ['prm'], 'b0')
            S.dma('pool', lambda e: e.dma_start(out=wd2[:], in_=wd2_d.ap()), [], ['wd2'], 'b1')
            S.dma('pool', lambda e: e.dma_start(out=wa2[:], in_=wa2_d.ap()), [], ['wa2'], 'b2')
            S.dma('sp', lambda e: e.dma_start(out=mskf[:], in_=msk_d.ap().rearrange("p (a t) -> p a t", a=4)), [], ['mskf'], 'b3')
            S.op('dve', lambda e: e.tensor_copy(out=mskb[:], in_=mskf[:, 3, :]), ['mskf'], ['mskb'])
            S.op('dve', lambda e: e.memset(Hm[:], 0.0), [], [('Hm', j_) for j_ in range(6)])
            S.op('dve', lambda e: e.memset(Hb[:], 0.0), [], [('Hb', j_) for j_ in range(6)])
            for nm_, tl2 in (('btE', btE_2), ('btO', btO_2), ('ktE', ktE_2), ('ktO', ktO_2)):
                for i_ in range(2):
                    S.op('pool', lambda e, tl=tl2[i_]: e.memset(tl[:], 0.0), [], [(nm_, i_)])
            EXPM05 = float(np.exp(-0.5))

            def D_(eng, fn, r, w):
                S.op(eng, fn, r, w)

            PTK = set(['logw', 'aa', 'cs', 'ep', 'em', 'epv', 'kkn', 'sqb', 'rn', 'kmod', 'tmpa', 'bon', 'ab', 'bt', 'ktl', 'rbr', 'vb16', 'btE', 'btO', 'ktE', 'ktO', 'vtok', 'wc', 'gAabT', 'gAab', 'gAak', 'gArb', 'gArk', 'Rm', 'Rb', 'Qa', 'QTa', 'Qn', 'QTn', 'Xb', 'Ub', 'ytok', 'yst', 'ynb', 'ysq', 'yfm'])

            def tile_body(c, c0, j):
                pbf = j % 2
                logw = logw_2[pbf]
                aa = aa_2[pbf]
                cs = cs_2[pbf]
                ep = ep_2[pbf]
                em = em_2[pbf]
                epv = epv_2[pbf]
                kkn = kkn_2[pbf]
                sqb = sqb_2[pbf]
                rn = rn_2[pbf]
                kmod = kmod_2[pbf]
                tmpa = tmpa_2[pbf]
                bon = bon_2[pbf]
                ab = ab_2[pbf]
                bt = bt_2[pbf]
                ktl = ktl_2[pbf]
                rbr = rbr_2[pbf]
                vb16 = vb16_2[pbf]
                btE = btE_2[pbf]
                btO = btO_2[pbf]
                ktE = ktE_2[pbf]
                ktO = ktO_2[pbf]
                vtok = vtok_2[pbf]
                wc = wc_2[pbf]
                gAabT = gAabT_2[pbf]
                gAab = gAab_2[pbf]
                gAak = gAak_2[pbf]
                gArb = gArb_2[pbf]
                gArk = gArk_2[pbf]
                Rm = Rm_2[pbf]
                Rb = Rb_2[pbf]
                Qa = Qa_2[pbf]
                QTa = QTa_2[pbf]
                Qn = Qn_2[pbf]
                QTn = QTn_2[pbf]
                Xb = Xb_2[pbf]
                Ub = Ub_2[pbf]
                ytok = ytok_2[pbf]
                yst = yst_2[pbf]
                ynb = ynb_2[pbf]
                ysq = ysq_2[pbf]
                yfm = yfm_2[pbf]

                def kx(k):
                    if isinstance(k, str):
                        if k in PTK:
                            return (k, pbf)
                        if k in ('Hm', 'Hb'):
                            return (k, j)
                    return k

                def D_(eng, fn, r, w):
                    S.op(eng, fn, [kx(k) for k in r], [kx(k) for k in w])
                D_('pe', lambda e, j=j: e.matmul(PS[0][:, 0:CH], lhsT=wd2[:, j * P:(j + 1) * P], rhs=lwb[:], start=True, stop=True),
                   ['wd2', 'lwb'], [('ps', 0)])
                D_('pe', lambda e, j=j: e.matmul(PS[0][:, CH:2 * CH], lhsT=wa2[:, j * P:(j + 1) * P], rhs=lab[:], start=True, stop=True),
                   ['wa2', 'lab'], [('ps', 0)])
                D_('act', lambda e, j=j: e.activation(out=logw[:], in_=PS[0][:, 0:CH], func=AF.Sigmoid, bias=prm[:, 0, j:j + 1]),
                   [('ps', 0), 'prm'], ['logw'])
                D_('act', lambda e, j=j: e.activation(out=aa[:], in_=PS[0][:, CH:2 * CH], func=AF.Sigmoid, bias=prm[:, 1, j:j + 1]),
                   [('ps', 0), 'prm'], ['aa'])
                D_('dve', lambda e: e.tensor_scalar(out=logw[:], in0=logw[:], scalar1=-EXPM05, scalar2=None, op0=ALU.mult),
                   ['logw'], ['logw'])
                D_('dve', lambda e: e.tensor_tensor_scan(out=cs[:], data0=onesf[:, 0:CH], data1=logw[:], initial=0.0,
                                                         op0=ALU.mult, op1=ALU.add), ['logw', 'onesf'], ['cs'])
                D_('act', lambda e: e.activation(out=ep[:], in_=cs[:], func=AF.Exp), ['cs'], ['ep'])
                D_('act', lambda e: e.activation(out=em[:], in_=cs[:], func=AF.Exp, scale=-1.0), ['cs'], ['em'])
                D_('dve', lambda e: e.tensor_sub(out=tmpa[:], in0=cs[:], in1=logw[:]), ['cs', 'logw'], ['tmpa'])
                D_('act', lambda e: e.activation(out=epv[:], in_=tmpa[:], func=AF.Exp), ['tmpa'], ['epv'])
                D_('dve', lambda e: e.tensor_copy(out=wc[:], in_=ep[:, CH - 1:CH]), ['ep'], ['wc'])
                D_('dve', lambda e, j=j: e.tensor_scalar(out=kkn[:], in0=kk_[:, j, :], scalar1=prm[:, 2, j:j + 1], scalar2=None,
                                                        op0=ALU.mult), ['kk_', 'prm'], ['kkn'])
                D_('dve', lambda e: e.tensor_mul(out=sqb[:], in0=kkn[:], in1=kkn[:]), ['kkn'], ['sqb'])
                D_('pe', lambda e: e.matmul(PS[1][:, 0:CH], lhsT=mskb[:], rhs=sqb[:], start=True, stop=True), ['mskb', 'sqb'], [('ps', 1)])
                D_('act', lambda e: e.activation(out=rn[:], in_=PS[1][:, 0:CH], func=AF.Sqrt), [('ps', 1)], ['rn'])
                D_('dve', lambda e: e.tensor_scalar_max(out=rn[:], in0=rn[:], scalar1=1e-12), ['rn'], ['rn'])
                D_('dve', lambda e: e.reciprocal(out=rn[:], in_=rn[:]), ['rn'], ['rn'])
                D_('dve', lambda e: e.tensor_mul(out=kkn[:], in0=kkn[:], in1=rn[:]), ['kkn', 'rn'], ['kkn'])
                D_('dve', lambda e, j=j: e.tensor_scalar(out=tmpa[:], in0=aa[:], scalar1=-1.0, scalar2=prm[:, 3, j:j + 1],
                                                        op0=ALU.add, op1=ALU.mult), ['aa', 'prm'], ['tmpa'])
                D_('dve', lambda e, j=j: e.scalar_tensor_tensor(out=kmod[:], in0=tmpa[:], scalar=1.0, in1=kk_[:, j, :],
                                                               op0=ALU.add, op1=ALU.mult), ['tmpa', 'kk_'], ['kmod'])
                D_('dve', lambda e: e.scalar_tensor_tensor(out=ab[:], in0=kkn[:], scalar=-1.0, in1=epv[:], op0=ALU.mult, op1=ALU.mult),
                   ['kkn', 'epv'], ['ab'])
                D_('dve', lambda e: e.tensor_mul(out=tmpa[:], in0=kkn[:], in1=aa[:]), ['kkn', 'aa'], ['tmpa'])
                D_('pool', lambda e: e.tensor_mul(out=bt[:], in0=tmpa[:], in1=em[:]), ['tmpa', 'em'], ['bt'])
                D_('pool', lambda e: e.tensor_mul(out=ktl[:], in0=kmod[:], in1=em[:]), ['kmod', 'em'], ['ktl'])
                D_('pool', lambda e, j=j: e.tensor_mul(out=rbr[:], in0=rr[:, j, :], in1=ep[:]), ['rr', 'ep'], ['rbr'])
                D_('pool', lambda e, j=j: e.tensor_copy(out=vb16[:], in_=vv[:, j, :]), ['vv'], ['vb16'])
                D_('dve', lambda e, j=j: e.scalar_tensor_tensor(out=sqb[:], in0=rr[:, j, :], scalar=prm[:, 4, j:j + 1], in1=kmod[:],
                                                               op0=ALU.mult, op1=ALU.mult), ['rr', 'kmod', 'prm', 'sqb'], ['sqb'])
                D_('pe', lambda e: e.matmul(PS[1][:, CH:2 * CH], lhsT=mskb[:], rhs=sqb[:], start=True, stop=True), ['mskb', 'sqb'], [('ps', 1)])
                D_('dve', lambda e, j=j: e.tensor_mul(out=bon[:], in0=PS[1][:, CH:2 * CH], in1=vv[:, j, :]), [('ps', 1), 'vv'], ['bon'])
                pv1 = PS[1].bitcast(BF16)
                D_('pe', lambda e, pv1=pv1: e.transpose(out=pv1[:, 512:640], in_=bt[:], identity=ident[:]), ['bt', 'ident'], [('ps', 1)])
                D_('pe', lambda e, pv1=pv1: e.transpose(out=pv1[:, 640:768], in_=ktl[:], identity=ident[:]), ['ktl', 'ident'], [('ps', 1)])
                D_('pe', lambda e, pv1=pv1: e.transpose(out=pv1[:, 768:896], in_=vb16[:], identity=ident[:]), ['vb16', 'ident'], [('ps', 1)])
                D_('act', lambda e, pv1=pv1: e.activation(out=btE[:, 0:64], in_=pv1[:, 512:576], func=AF.Copy), [('ps', 1)], ['btE'])
                D_('act', lambda e, pv1=pv1: e.activation(out=btO[:, 64:128], in_=pv1[:, 576:640], func=AF.Copy), [('ps', 1)], ['btO'])
                D_('act', lambda e, pv1=pv1: e.activation(out=ktE[:, 0:64], in_=pv1[:, 640:704], func=AF.Copy), [('ps', 1)], ['ktE'])
                D_('act', lambda e, pv1=pv1: e.activation(out=ktO[:, 64:128], in_=pv1[:, 704:768], func=AF.Copy), [('ps', 1)], ['ktO'])
                D_('act', lambda e, pv1=pv1: e.activation(out=vtok[:], in_=pv1[:, 768:896], func=AF.Copy), [('ps', 1)], ['vtok'])
                for h2 in range(2):
                    sl = slice(h2 * 64, (h2 + 1) * 64)
                    grams = ((bt, ab, 0), (ab, bt, 1), (ktl, ab, 2), (bt, rbr, 3))
                    for (L, R_, gi) in grams:
                        D_('pe', lambda e, L=L, R_=R_, gi=gi, sl=sl: e.matmul(PS[2][:, gi * P:(gi + 1) * P], lhsT=L[sl, :], rhs=R_[sl, :],
                                                                             start=True, stop=True),
                           ['ab', 'bt', 'ktl', 'rbr'], [('ps', 2)])
                    D_('pe', lambda e, sl=sl: e.matmul(PS[3][:, 0:P], lhsT=ktl[sl, :], rhs=rbr[sl, :], start=True, stop=True),
                       ['ktl', 'rbr'], [('ps', 3)])
                    D_('dve', lambda e, h2=h2: e.tensor_mul(out=gAabT[:, h2, :], in0=PS[2][:, 0:P], in1=mskf[:, 0, :]), [('ps', 2), 'mskf'], ['gAabT'])
                    D_('dve', lambda e, h2=h2: e.tensor_mul(out=gAab[:, h2, :], in0=PS[2][:, P:2 * P], in1=mskf[:, 2, :]), [('ps', 2), 'mskf'], ['gAab'])
                    D_('dve', lambda e, h2=h2: e.tensor_mul(out=gAak[:, h2, :], in0=PS[2][:, 2 * P:3 * P], in1=mskf[:, 0, :]), [('ps', 2), 'mskf'], ['gAak'])
                    D_('dve', lambda e, h2=h2: e.tensor_mul(out=gArb[:, h2, :], in0=PS[2][:, 3 * P:4 * P], in1=mskf[:, 1, :]), [('ps', 2), 'mskf'], ['gArb'])
                    D_('dve', lambda e, h2=h2: e.tensor_mul(out=gArk[:, h2, :], in0=PS[3][:, 0:P], in1=mskf[:, 1, :]), [('ps', 3), 'mskf'], ['gArk'])
                D_('dve', lambda e: e.tensor_mul(out=Rm[:], in0=gAabT[:], in1=mskf[:, 0:1, :].to_broadcast([P, 2, P])), ['gAabT', 'mskf'], ['Rm'])
                for h2 in range(2):
                    D_('dve', lambda e, h2=h2: e.tensor_add(out=Rm[:, h2, :], in0=Rm[:, h2, :], in1=identf[:]), ['Rm', 'identf'], ['Rm'])
                D_('pool', lambda e: e.tensor_copy(out=Rb[:], in_=Rm[:]), ['Rm'], ['Rb'])
                Qc, QTc, Qx, QTx = gAabT, gAab, Qa, QTa
                kq = {id(gAabT): 'gAabT', id(gAab): 'gAab', id(Qa): 'Qa', id(QTa): 'QTa', id(Qn): 'Qn', id(QTn): 'QTn'}
                for lev in range(1, 7):
                    for h2 in range(2):
                        D_('pe', lambda e, h2=h2, Qc=Qc, QTc=QTc: e.matmul(PS[4][:, h2 * P:(h2 + 1) * P], lhsT=QTc[:, h2, :], rhs=Qc[:, h2, :],
                                                                         start=True, stop=True), [kq[id(Qc)], kq[id(QTc)]], [('ps', 4)])
                        D_('pe', lambda e, h2=h2, Qc=Qc, QTc=QTc: e.matmul(PS[4][:, (2 + h2) * P:(3 + h2) * P], lhsT=Qc[:, h2, :], rhs=QTc[:, h2, :],
                                                                         start=True, stop=True), [kq[id(Qc)], kq[id(QTc)]], [('ps', 4)])
                    D_('act', lambda e, Qx=Qx: e.activation(out=Qx[:], in_=PS[4][:, 0:2 * P].rearrange("p (a t) -> p a t", a=2), func=AF.Copy),
                       [('ps', 4)], [kq[id(Qx)]])
                    D_('act', lambda e, QTx=QTx: e.activation(out=QTx[:], in_=PS[4][:, 2 * P:4 * P].rearrange("p (a t) -> p a t", a=2), func=AF.Copy),
                       [('ps', 4)], [kq[id(QTx)]])
                    for h2 in range(2):
                        D_('pe', lambda e, h2=h2, QTx=QTx: e.matmul(PS[5][:, h2 * P:(h2 + 1) * P], lhsT=QTx[:, h2, :], rhs=Rb[:, h2, :],
                                                                  start=True, stop=True), [kq[id(QTx)], 'Rb'], [('ps', 5)])
                    D_('dve', lambda e: e.tensor_add(out=Rm[:], in0=Rm[:], in1=PS[5][:, 0:2 * P].rearrange("p (a t) -> p a t", a=2)),
                       [('ps', 5), 'Rm'], ['Rm'])
                    D_('pool', lambda e: e.tensor_copy(out=Rb[:], in_=Rm[:]), ['Rm'], ['Rb'])
                    Qc, QTc = Qx, QTx
                    Qx, QTx = (Qn, QTn) if Qx is Qa else (Qa, QTa)
                for h2 in range(2):
                    sl = slice(h2 * 64, (h2 + 1) * 64)
                    vs = slice(h2 * 64, (h2 + 1) * 64)
                    D_('pe', lambda e, h2=h2, vs=vs: e.matmul(PS[6][:, h2 * 64:(h2 + 1) * 64], lhsT=gAak[:, h2, :], rhs=vtok[:, vs],
                                                            start=True, stop=False), ['gAak', 'vtok'], [('ps', 6)])
                    D_('pe', lambda e, h2=h2, sl=sl, j=j: e.matmul(PS[6][:, h2 * 64:(h2 + 1) * 64], lhsT=ab[sl, :], rhs=Hb[sl, j, :],
                                                                 start=False, stop=True), ['ab', 'Hb'], [('ps', 6)])
                D_('act', lambda e: e.activation(out=Xb[:], in_=PS[6][:, 0:128].rearrange("p (a t) -> p a t", a=2), func=AF.Copy),
                   [('ps', 6)], ['Xb'])
                for h2 in range(2):
                    D_('pe', lambda e, h2=h2: e.matmul(PS[6][:, 128 + h2 * 64:128 + (h2 + 1) * 64], lhsT=Rb[:, h2, :], rhs=Xb[:, h2, :],
                                                     start=True, stop=True), ['Rb', 'Xb'], [('ps', 6)])
                D_('act', lambda e: e.activation(out=Ub[:], in_=PS[6][:, 128:256].rearrange("p (a t) -> p a t", a=2), func=AF.Copy),
                   [('ps', 6)], ['Ub'])
                for h2 in range(2):
                    sl = slice(h2 * 64, (h2 + 1) * 64)
                    D_('pe', lambda e, h2=h2: e.matmul(PS[7][:, h2 * 64:(h2 + 1) * 64], lhsT=gArb[:, h2, :], rhs=Ub[:, h2, :],
                                                     start=True, stop=False), ['gArb', 'Ub'], [('ps', 7)])
                    D_('pe', lambda e, h2=h2, sl=sl: e.matmul(PS[7][:, h2 * 64:(h2 + 1) * 64], lhsT=gArk[:, h2, :], rhs=vtok[:, sl],
                                                            start=False, stop=False), ['gArk', 'vtok'], [('ps', 7)])
                    D_('pe', lambda e, h2=h2, sl=sl, j=j: e.matmul(PS[7][:, h2 * 64:(h2 + 1) * 64], lhsT=rbr[sl, :], rhs=Hb[sl, j, :],
                                                                 start=False, stop=True), ['rbr', 'Hb'], [('ps', 7)])
                D_('pe', lambda e: e.matmul(PS[7][:, 128:192], lhsT=btE[:], rhs=Ub[:, 0, :], start=True, stop=False), ['btE', 'Ub'], [('ps', 7)])
                D_('pe', lambda e: e.matmul(PS[7][:, 128:192], lhsT=btO[:], rhs=Ub[:, 1, :], start=False, stop=False), ['btO', 'Ub'], [('ps', 7)])
                D_('pe', lambda e: e.matmul(PS[7][:, 128:192], lhsT=ktE[:], rhs=vtok[:, 0:64], start=False, stop=False), ['ktE', 'vtok'], [('ps', 7)])
                D_('pe', lambda e: e.matmul(PS[7][:, 128:192], lhsT=ktO[:], rhs=vtok[:, 64:128], start=False, stop=True), ['ktO', 'vtok'], [('ps', 7)])
                D_('dve', lambda e: e.tensor_copy(out=ytok[:], in_=PS[7][:, 0:128].rearrange("p (a t) -> p a t", a=2)), [('ps', 7)], ['ytok'])
                D_('dve', lambda e, j=j: e.tensor_add(out=Hm[:, j, :], in0=Hm[:, j, :], in1=PS[7][:, 128:192]), [('ps', 7), 'Hm'], ['Hm'])
                D_('dve', lambda e, j=j: e.tensor_scalar(out=Hm[:, j, :], in0=Hm[:, j, :], scalar1=wc[:, 0:1], scalar2=None, op0=ALU.mult),
                   ['Hm', 'wc'], ['Hm'])
                D_('pool', lambda e, j=j: e.tensor_copy(out=Hb[:, j, :], in_=Hm[:, j, :]), ['Hm'], ['Hb'])
                D_('dve', lambda e: e.tensor_reduce(out=yst[:, :, 0], in_=ytok[:], axis=AX.X, op=ALU.add), ['ytok'], ['yst'])
                D_('dve', lambda e: e.tensor_scalar(out=yst[:, :, 0], in0=yst[:, :, 0], scalar1=1.0 / 64, scalar2=None, op0=ALU.mult), ['yst'], ['yst'])
                D_('dve', lambda e: e.tensor_sub(out=ytok[:], in0=ytok[:], in1=yst[:, :, 0:1].to_broadcast([P, 2, 64])), ['ytok', 'yst'], ['ytok'])
                D_('dve', lambda e: e.tensor_mul(out=ysq[:], in0=ytok[:], in1=ytok[:]), ['ytok'], ['ysq'])
                D_('dve', lambda e: e.tensor_reduce(out=yst[:, :, 1], in_=ysq[:], axis=AX.X, op=ALU.add), ['ysq', 'yst'], ['yst'])
                D_('dve', lambda e: e.tensor_scalar(out=yst[:, :, 1], in0=yst[:, :, 1], scalar1=1.0 / 64, scalar2=LNX_EPS, op0=ALU.mult, op1=ALU.add),
                   ['yst'], ['yst'])
                D_('act', lambda e: e.activation(out=yst[:, :, 2], in_=yst[:, :, 1], func=AF.Sqrt), ['yst'], ['yst'])
                D_('dve', lambda e: e.reciprocal(out=yst[:, :, 3], in_=yst[:, :, 2]), ['yst'], ['yst'])
                D_('dve', lambda e: e.tensor_mul(out=ynb[:].rearrange("p (a t) -> p a t", a=2), in0=ytok[:],
                                                 in1=yst[:, :, 3:4].to_broadcast([P, 2, 64])), ['ytok', 'yst'], ['ynb'])
                pv3 = PS[3].bitcast(BF16)
                D_('pe', lambda e, pv3=pv3: e.transpose(out=pv3[:, 512:640], in_=ynb[:], identity=ident[:]), ['ynb', 'ident'], [('ps', 3)])
                D_('dve', lambda e, j=j, pv3=pv3: e.tensor_scalar(out=yfm[:], in0=pv3[:, 512:640], scalar1=prm[:, 5, j:j + 1],
                                                                 scalar2=prm[:, 6, j:j + 1], op0=ALU.mult, op1=ALU.add),
                   [('ps', 3), 'prm'], ['yfm'])
                D_('dve', lambda e: e.tensor_add(out=yfm[:], in0=yfm[:], in1=bon[:]), ['yfm', 'bon'], ['yfm'])
                ob = (c * 6 + j) % 2
                D_('pool', lambda e, ob=ob, j=j: e.tensor_mul(out=yob[ob][:], in0=yfm[:], in1=zz[:, j, :]), ['yfm', 'zz'], [('yob', ob)])
                S.dma('sp', lambda e, ob=ob, j=j, c0=c0: e.dma_start(out=yown.ap()[CW + j * P:CW + (j + 1) * P, c0:c0 + CH], in_=yob[ob][:]),
                      [('yob', ob)], [('yownb', j, c)], 'yob%d' % ob)

            for c in range(T // CH):
                c0 = c * CH
                for (nm_, tl, src) in (('rr', rr, scr['rb']), ('kk_', kk_, scr['kb']), ('vv', vv, scr['vb']), ('zz', zz, scr['zb'])):
                    S.dma('sp', lambda e, tl=tl, src=src, c0=c0: e.dma_start(
                        out=tl[:], in_=src.ap()[:, c0:c0 + CH].rearrange("(j p) t -> p j t", p=P)), [], [nm_], 'b_' + nm_)
                S.dma('sp', lambda e, c0=c0: e.dma_start(out=lwt[:], in_=scr['lw'].ap()[:, c0:c0 + CH]), [], ['lwt'], 'b_lw')
                S.dma('sp', lambda e, c0=c0: e.dma_start(out=lat[:], in_=scr['la'].ap()[:, c0:c0 + CH]), [], ['lat'], 'b_la')
                D_('act', lambda e: e.activation(out=lwb[:], in_=lwt[:], func=AF.Tanh), ['lwt'], ['lwb'])
                D_('dve', lambda e: e.tensor_copy(out=lab[:], in_=lat[:]), ['lat'], ['lab'])
                for j in range(6):
                    tile_body(c, c0, j)
        S.barrier()

    if stage >= 3:
        with ExitStack() as ph:
            zt = ph.enter_context(nc.sbuf_tensor("zt", [P, T], BF16))
            S.op('dve', lambda e: e.memset(zt[:], 0.0), [], ['zt'])
            for j in range(NYZ):
                j = j + (6 if NYZ == 6 else 0)
                S.dma('sp', lambda e, j=j: e.dma_start(out=yown.ap()[j * P:(j + 1) * P, :], in_=zt[:]),
                      ['zt'], [('yown', j)], 'yz')
        S.barrier()
        for j in range(D // P):
            S.dma('pool', lambda e, j=j: e.collective_compute(
                "AllGather", ALU.bypass, replica_groups=[[0, 1], [2, 3], [4, 5], [6, 7]],
                ins=[yown.ap()[j * P:(j + 1) * P, :].opt()], outs=[ygs[j].ap().opt()]), [], [('yg', j)], 'cc', inc=1)
        S.barrier()

    if stage >= 4:
        wpa_d = din("wpa", [1536, D])
        wpb_d = din("wpb", [1536, D])
        wpc_d = din("wpc", [1024, D])
        wo_d = din("wo", [D, D])
        xo_d = din("x_own", [T // 2, D])
        mT_d = dscr("s_mT", [D, T // 2], BF16)
        TH = T // 2
        chunks = []
        for j in range(12):
            chunks.append((wpa_d, j, (j // 6, (j % 6))))
        for j in range(12):
            chunks.append((wpb_d, j, (j // 6, 6 + (j % 6))))
        for j in range(8):
            chunks.append((wpc_d, j, (j // 4, 12 + (j % 4))))
        with ExitStack() as ph:
            ysel = ph.enter_context(nc.sbuf_tensor("ysel", [P, 32, TH], BF16))
            hv = ph.enter_context(nc.sbuf_tensor("hv", [P, 2], F32))
            t0 = [ph.enter_context(nc.sbuf_tensor("t0_%d" % i, [P, TH], BF16)) for i in range(2)]
            t1 = [ph.enter_context(nc.sbuf_tensor("t1_%d" % i, [P, TH], BF16)) for i in range(2)]
            wt = [ph.enter_context(nc.sbuf_tensor("wt%d" % i, [P, 32, P], BF16)) for i in range(2)]
            g0 = [ph.enter_context(nc.sbuf_tensor("g0_%d" % i, [P, 3, TB], BF16)) for i in range(2)]
            g1 = [ph.enter_context(nc.sbuf_tensor("g1_%d" % i, [P, 3, TB], BF16)) for i in range(2)]
            gs = [ph.enter_context(nc.sbuf_tensor("gs_%d" % i, [P, 3, TB], F32)) for i in range(2)]
            ma = [ph.enter_context(nc.sbuf_tensor("ma_%d" % i, [P, TB], F32)) for i in range(2)]
            mb = [ph.enter_context(nc.sbuf_tensor("mb_%d" % i, [P, TB], F32)) for i in range(2)]
            mo = [ph.enter_context(nc.sbuf_tensor("mo_%d" % i, [P, TB], BF16)) for i in range(2)]
            S.dma('sp', lambda e: e.dma_start(out=hv[:], in_=hhv_d.ap()), [], ['hv'], 'c3')
            for kc, (wd, wj, yrow) in enumerate(chunks):
                b = kc % 2
                rk, cj = yrow
                S.dma('sp', lambda e, b=b, rk=rk, cj=cj: e.dma_start(out=t0[b][:], in_=ygs[cj].ap()[rk * P:(rk + 1) * P, 0:TH]),
                      [], [('t0', b)], 't0%d' % b)
                S.dma('sp', lambda e, b=b, rk=rk, cj=cj: e.dma_start(out=t1[b][:], in_=ygs[cj].ap()[rk * P:(rk + 1) * P, TH:T]),
                      [], [('t1', b)], 't1%d' % b)
                S.op('dve', lambda e, b=b: e.tensor_scalar(out=t0[b][:], in0=t0[b][:], scalar1=hv[:, 0:1], scalar2=None,
                                                           op0=ALU.mult), [('t0', b), 'hv'], [('t0', b)])
                S.op('dve', lambda e, b=b, kc=kc: e.scalar_tensor_tensor(
                    out=ysel[:, kc, :], in0=t1[b][:], scalar=hv[:, 1:2], in1=t0[b][:], op0=ALU.mult, op1=ALU.add),
                    [('t0', b), ('t1', b), 'hv'], [('ysel', kc)])
            ysel_keys = [('ysel', kc) for kc in range(32)]
            it = 0
            for ct in range(D // P):
                wi = ct % 2
                for (wd, j0, n) in ((wpa_d, 0, 12), (wpb_d, 12, 12), (wpc_d, 24, 8)):
                    srcv = wd.ap()[:, ct * P:(ct + 1) * P].rearrange("(k p) c -> p k c", p=P)
                    S.dma('pool', lambda e, wi=wi, j0=j0, n=n, srcv=srcv: e.dma_start(out=wt[wi][:, j0:j0 + n, :], in_=srcv),
                          [], [('wt', wi)], 'wt%d_%d' % (wi, j0))
                for tb in range(TH // TB):
                    bi = it % 2
                    it += 1
                    pa, pb_, pc = PS[(it % 2) * 3 + 0], PS[(it % 2) * 3 + 1], PS[(it % 2) * 3 + 2]
                    pk = [('ps', (it % 2) * 3 + i) for i in range(3)]
                    for (pt, key, j0, n) in ((pa, pk[0], 0, 12), (pb_, pk[1], 12, 12), (pc, pk[2], 24, 8)):
                        for j in range(n):
                            S.op('pe', lambda e, pt=pt, j=j, j0=j0, n=n, tb=tb, wi=wi: e.matmul(
                                pt[:, :], lhsT=wt[wi][:, j0 + j, :], rhs=ysel[:, j0 + j, tb * TB:(tb + 1) * TB],
                                start=(j == 0), stop=(j == n - 1)),
                                [('wt', wi)] + (ysel_keys if j == 0 else []), [key], sig=(j == n - 1))
                    for i3 in range(3):
                        r0 = i3 * D + ct * P
                        S.dma('sp', lambda e, bi=bi, i3=i3, r0=r0, tb=tb: e.dma_start(
                            out=g0[bi][:, i3, :], in_=scr['g'].ap()[r0:r0 + P, tb * TB:(tb + 1) * TB]),
                            [], [('g0', bi)], 'g0%d' % bi)
                        S.dma('sp', lambda e, bi=bi, i3=i3, r0=r0, tb=tb: e.dma_start(
                            out=g1[bi][:, i3, :], in_=scr['g'].ap()[r0:r0 + P, TH + tb * TB:TH + (tb + 1) * TB]),
                            [], [('g1', bi)], 'g1%d' % bi)
                    S.op('pool', lambda e, bi=bi: e.tensor_scalar(out=gs[bi][:], in0=g0[bi][:], scalar1=hv[:, 0:1],
                                                                  scalar2=None, op0=ALU.mult),
                         [('g0', bi), 'hv'], [('gs', bi)])
                    S.op('dve', lambda e, bi=bi: e.scalar_tensor_tensor(
                        out=gs[bi][:], in0=g1[bi][:], scalar=hv[:, 1:2], in1=gs[bi][:], op0=ALU.mult, op1=ALU.add),
                        [('g1', bi), ('gs', bi), 'hv'], [('gs', bi)])
                    S.op('dve', lambda e, bi=bi, pa=pa: e.tensor_mul(out=ma[bi][:], in0=pa[:, :], in1=gs[bi][:, 0, :]),
                         [pk[0], ('gs', bi)], [('ma', bi)])
                    S.op('dve', lambda e, bi=bi, pb_=pb_: e.tensor_mul(out=mb[bi][:], in0=pb_[:, :], in1=gs[bi][:, 1, :]),
                         [pk[1], ('gs', bi)], [('mb', bi)])
                    S.op('pool', lambda e, bi=bi: e.tensor_add(out=ma[bi][:], in0=ma[bi][:], in1=mb[bi][:]),
                         [('ma', bi), ('mb', bi)], [('ma', bi)])
                    S.op('dve', lambda e, bi=bi, pc=pc: e.tensor_mul(out=mb[bi][:], in0=pc[:, :], in1=gs[bi][:, 2, :]),
                         [pk[2], ('gs', bi), ('ma', bi)], [('mb', bi)])
                    S.op('pool', lambda e, bi=bi: e.tensor_add(out=mo[bi][:], in0=ma[bi][:], in1=mb[bi][:]),
                         [('ma', bi), ('mb', bi)], [('mo', bi)])
                    S.dma('sp', lambda e, bi=bi, ct=ct, tb=tb: e.dma_start(
                        out=mT_d.ap()[ct * P:(ct + 1) * P, tb * TB:(tb + 1) * TB], in_=mo[bi][:]),
                        [('mo', bi)], [('mT', ct, tb)], 'mo%d' % bi)
        S.barrier()
        with ExitStack() as ph:
            mT = ph.enter_context(nc.sbuf_tensor("mT_sb", [P, 16, TH], BF16))
            wo = ph.enter_context(nc.sbuf_tensor("wo_sb", [P, 16, D], BF16))
            fgb = ph.enter_context(nc.sbuf_tensor("fgb", [P, D], F32))
            xo = [ph.enter_context(nc.sbuf_tensor("xo%d" % i, [P, D], F32)) for i in range(2)]
            ro = [ph.enter_context(nc.sbuf_tensor("ro%d" % i, [P, D], F32)) for i in range(2)]
            jk = ph.enter_context(nc.sbuf_tensor("jk", [P, D], BF16))
            s2 = [ph.enter_context(nc.sbuf_tensor("s2_%d" % i, [P, 2], F32)) for i in range(2)]
            S.dma('sp', lambda e: e.dma_start(out=fgb[:], in_=fng_d.ap().partition_broadcast(P)), [], ['fgb'], 'c4')
            for k in range(16):
                S.dma('sp', lambda e, k=k: e.dma_start(out=mT[:, k, :], in_=mT_d.ap()[k * P:(k + 1) * P, :]),
                      [], ['mTs'], 'mTs')
            for q in range(4):
                S.dma('pool', lambda e, q=q: e.dma_start(
                    out=wo[:, :, q * 512:(q + 1) * 512],
                    in_=wo_d.ap()[:, q * 512:(q + 1) * 512].rearrange("(k p) c -> p k c", p=P)),
                    [], ['wos'], 'wos')
            for tt in range(TH // P):
                b = tt % 2
                S.dma('sp', lambda e, b=b, tt=tt: e.dma_start(out=xo[b][:], in_=xo_d.ap()[tt * P:(tt + 1) * P, :]),
                      [], [('xo', b)], 'xo%d' % b)
                for n in range(4):
                    pi = (tt % 2) * 4 + n
                    for k in range(16):
                        S.op('pe', lambda e, pi=pi, k=k, n=n, tt=tt: e.matmul(
                            PS[pi][:, :], lhsT=mT[:, k, tt * P:(tt + 1) * P], rhs=wo[:, k, n * 512:(n + 1) * 512],
                            start=(k == 0), stop=(k == 15)), ['mTs', 'wos'] if k == 0 else [], [('ps', pi)], sig=(k == 15))
                    S.op('dve', lambda e, pi=pi, b=b, n=n: e.tensor_add(
                        out=ro[b][:, n * 512:(n + 1) * 512], in0=PS[pi][:, :], in1=xo[b][:, n * 512:(n + 1) * 512]),
                        [('ps', pi), ('xo', b)], [('ro', b)])
                S.op('act', lambda e, b=b: e.activation(out=jk[:], in_=ro[b][:], func=AF.Square, accum_out=s2[b][:, 0:1]),
                     [('ro', b)], [('s2', b), 'jk'])
                S.op('dve', lambda e, b=b: e.tensor_scalar(out=s2[b][:, 1:2], in0=s2[b][:, 0:1], scalar1=1.0 / D,
                                                           scalar2=RMS_EPS, op0=ALU.mult, op1=ALU.add),
                     [('s2', b)], [('s2', b)])
                S.op('act', lambda e, b=b: e.activation(out=s2[b][:, 1:2], in_=s2[b][:, 1:2], func=AF.Sqrt),
                     [('s2', b)], [('s2', b)])
                S.op('dve', lambda e, b=b: e.reciprocal(out=s2[b][:, 1:2], in_=s2[b][:, 1:2]), [('s2', b)], [('s2', b)])
                S.op('dve', lambda e, b=b: e.scalar_tensor_tensor(out=ro[b][:], in0=ro[b][:], scalar=s2[b][:, 1:2],
                                                                  in1=fgb[:], op0=ALU.mult, op1=ALU.mult),
                     [('ro', b), ('s2', b), 'fgb'], [('ro', b)])
                S.dma('sp', lambda e, b=b, tt=tt: e.dma_start(out=out_d.ap()[tt * P:(tt + 1) * P, :], in_=ro[b][:]),
                      [('ro', b)], [('out', tt)], 'out%d' % b)
    S.finish()
    S.emit()
    return nc, dbg_out


def _t5_bucket(d):
    import math
    n = max(int(d), 0)
    if n < 16:
        return n
    v = 16 + int(np.float32(np.log(np.float32(n) / np.float32(16)) / np.float32(math.log(128 / 16)) * np.float32(16)))
    return min(v, 31)


def _consts():
    oh = np.zeros((33, 640), np.float32)
    for j in range(640):
        d = j - 255
        if d < 0:
            oh[32, j] = 1.0
        else:
            oh[_t5_bucket(d), j] = 1.0
    negm = np.zeros((32, 16), np.float32)
    for qt in range(32):
        for n in range(16):
            if n >= qt // 2:
                negm[qt, n] = -1e30
    negm = np.ascontiguousarray(np.broadcast_to(negm.reshape(1, 512), (128, 512)))
    E = np.zeros((16, 16, 128), np.float32)
    for n in range(16):
        E[n, n, :] = 30000.0
    return oh, negm, E.reshape(16, 16 * 128)


_OH, _NEGM, _EIN = _consts()
_pp = np.arange(128)[:, None]
_ff = np.arange(128)[None, :]
_MSK = np.ascontiguousarray(np.concatenate([(_pp < _ff), (_pp <= _ff), (_pp > _ff), ((_pp // 64) == (_ff // 64))], 1).astype(np.float32))


def _tile_cols(v, ntile):
    return np.ascontiguousarray(np.asarray(v, np.float32).reshape(ntile, 128).T)


def make_in_maps(inp):
    maps = []
    w_in = inp['w_in'][0]
    offs = np.cumsum([0, 3 * 1536, 1536, 3 * 1536, 1536, 96, 96, 1024, 1024, 3 * 2048])
    o_qkv_a, o_z_a, o_rkv_b, o_z_b, o_lw, o_la, o_q_c, o_z_c, o_g = offs[:9]
    ident = np.eye(128, dtype=np.float32)
    for c in range(8):
        b, hh = c // 2, c % 2
        a0 = hh * CW
        cc0 = hh * CC
        cols = np.concatenate([
            np.arange(o_qkv_a + a0, o_qkv_a + a0 + CW),
            np.arange(o_qkv_a + 1536 + a0, o_qkv_a + 1536 + a0 + CW),
            np.arange(o_qkv_a + 3072 + a0, o_qkv_a + 3072 + a0 + CW),
            np.arange(o_z_a + a0, o_z_a + a0 + CW),
            np.arange(o_rkv_b + a0, o_rkv_b + a0 + CW),
            np.arange(o_rkv_b + 1536 + a0, o_rkv_b + 1536 + a0 + CW),
            np.arange(o_rkv_b + 3072 + a0, o_rkv_b + 3072 + a0 + CW),
            np.arange(o_z_b + a0, o_z_b + a0 + CW),
            np.arange(o_lw, o_lw + 96), np.arange(o_la, o_la + 96),
            np.arange(o_q_c + cc0, o_q_c + cc0 + CC),
            np.arange(o_z_c + cc0, o_z_c + cc0 + CC)])
        m = {}
        m['x'] = np.ascontiguousarray(inp['x'][b])
        m['mem'] = np.ascontiguousarray(inp['mem'][b])
        m['w1'] = np.ascontiguousarray(w_in[:, cols])
        m['wg'] = np.ascontiguousarray(w_in[:, o_g:o_g + 3 * D])
        m['norm_g'] = np.ascontiguousarray(inp['norm_g'][0][None, :])
        m['mem_norm_g'] = np.ascontiguousarray(inp['mem_norm_g'][0][None, :])
        m['final_norm_g'] = np.ascontiguousarray(inp['final_norm_g'][None, :])
        m["ident_in"] = ident
        m["ones_in"] = np.ones((128, 128), np.float32)
        m["oh_in"] = _OH
        prm = np.zeros((128, 7, 6), np.float32)
        for ai, nm_ in enumerate(('rw_w0', 'rw_a0', 'rw_k_k', 'rw_k_a', 'rw_r_k', 'rw_lnx_w', 'rw_lnx_b')):
            vfull = np.asarray(inp[nm_][0], np.float32).reshape(-1)
            prm[:, ai, :] = _tile_cols(vfull[a0:a0 + CW], 6)
        m["prm"] = prm.reshape(128, 42)
        m["wd2"] = np.ascontiguousarray(inp['rw_w_decay2'][0][:, a0:a0 + CW])
        m["wa2"] = np.ascontiguousarray(inp['rw_w_aaa2'][0][:, a0:a0 + CW])
        m["msk_in"] = _MSK
        m["negm_in"] = _NEGM
        m["E_in"] = _EIN
        m["relb"] = np.ascontiguousarray(inp['rel_bias'][:, hh * NH_A:(hh + 1) * NH_A])
        wm = inp['w_mem_kv'][0]
        m['wmkv'] = np.ascontiguousarray(np.concatenate([wm[:, cc0:cc0 + CC], wm[:, 1024 + cc0:1024 + cc0 + CC]], 1))
        hv = np.zeros((128, 2), np.float32); hv[:, 0] = 1 - hh; hv[:, 1] = hh
        m['hhv'] = hv
        m['wpa'] = np.ascontiguousarray(inp['w_proj_a'][0])
        m['wpb'] = np.ascontiguousarray(inp['w_proj_b'][0])
        m['wpc'] = np.ascontiguousarray(inp['w_proj_c'][0])
        m['wo'] = np.ascontiguousarray(inp['w_out'][0])
        m['x_own'] = np.ascontiguousarray(inp['x'][b][hh * (T // 2):(hh + 1) * (T // 2)])
        mu1 = np.zeros((128, 20), np.float32)
        mu1[:, 0:6] = _tile_cols(inp['rw_mu_r'][0][a0:a0 + CW], 6)
        mu1[:, 6:12] = _tile_cols(inp['rw_mu_k'][0][a0:a0 + CW], 6)
        mu1[:, 12:18] = _tile_cols(inp['rw_mu_v'][0][a0:a0 + CW], 6)
        mu1[:96, 18] = inp['rw_mu_w'][0]
        mu1[:96, 19] = inp['rw_mu_a'][0]
        m['mu1'] = mu1
        maps.append(m)
    return maps


_NC_CACHE = {}


def kernel(**inputs):
    if 'nc' not in _NC_CACHE:
        import os
        _NC_CACHE['nc'] = build(stage=int(os.environ.get('K_STAGE', '99')), dbg=False)[0]
    nc = _NC_CACHE['nc']
    maps = make_in_maps(inputs)
    used = set(t.name for t in nc.m.functions[0].allocs) if False else set(t for t in ('x', 'mem', 'w1', 'wg', 'norm_g', 'mem_norm_g', 'final_norm_g', 'ident_in', 'hhv', 'mu1',
                           'wpa', 'wpb', 'wpc', 'wo', 'x_own', 'ones_in', 'wmkv', 'oh_in', 'negm_in', 'E_in', 'relb', 'prm', 'wd2', 'wa2', 'msk_in'))
    maps = [{k: v for k, v in m.items() if k in used} for m in maps]
    res = run_bass_kernel_spmd(nc, maps, core_ids=list(range(8)))
    out = np.zeros((4, T, D), np.float32)
    for c in range(8):
        b, hh = c // 2, c % 2
        out[b, hh * (T // 2):(hh + 1) * (T // 2)] = np.asarray(res.results[c]['out'], np.float32)
    return out
```
